# Optimizing a Trainium2 kernel written in Bass

```python
import math
import jax, jax.numpy as jnp
from jax import lax
import numpy as np

D_MODEL = 2048
BATCH = 8
SEQ = 4096
DEPTH = 1

HEAD_DIM = 128
ATTN_HEADS = D_MODEL // (2 * HEAD_DIM)
ATTN_W = ATTN_HEADS * HEAD_DIM
HY_GROUP_DIM = 128
HY_GROUPS = D_MODEL // (2 * HY_GROUP_DIM)
HY_W = HY_GROUPS * HY_GROUP_DIM
IN_W = 3 * ATTN_W + 3 * HY_W
ATTN_PATTERNS = ((128, 1), (512, 4), (2048, 16))
N_BUCKETS = 32
REL_MAX_DIST = 1024
HY_EMB = 33
HY_FILTER_WIDTH = 64
HY_INNER = 2
HY_SHORT_CONV = 3
HY_DECAY_MIN = 3.07
HY_DECAY_MAX = 15.35
FFN_HIDDEN = ((8 * D_MODEL // 3 + 255) // 256) * 256
PLE_DIM = 256
EPS = 1e-6
NEG = -1e30

kernel_name = "hybrid_hyena_dilated_attn_encoder_layer"


def rms_norm(x, gain):
    xf = x.astype(jnp.float32)
    y = xf * lax.rsqrt(jnp.mean(xf * xf, axis=-1, keepdims=True) + EPS)
    return (y * gain.astype(jnp.float32)).astype(x.dtype)


def group_rms_norm(y, gain, n_groups):
    B, S, W = y.shape
    yg = rms_norm(y.reshape(B, S, n_groups, W // n_groups), gain.reshape(n_groups, W // n_groups))
    return yg.reshape(B, S, W)


def t5_bucket(rel):
    half = N_BUCKETS // 2
    exact = half // 2
    n = jnp.abs(rel)
    large = exact + (jnp.log(jnp.maximum(n, 1).astype(jnp.float32) / exact)
                     / math.log(REL_MAX_DIST / exact) * (half - exact)).astype(jnp.int32)
    large = jnp.minimum(large, half - 1)
    return jnp.where(rel > 0, half, 0) + jnp.where(n < exact, n, large)


def dilated_window_attention(q, k, v, rel_bias, window, dilation):
    B, S, H, Dh = q.shape
    d = dilation
    n = (window // 2) // d
    Q = n
    Ls = S // d
    nb = -(-Ls // Q)
    Lp = nb * Q

    def strided(t):
        return t.reshape(B, Ls, d, H, Dh).transpose(0, 2, 3, 1, 4)

    qs = jnp.pad(strided(q), ((0, 0), (0, 0), (0, 0), (0, Lp - Ls), (0, 0)))
    pad_kv = ((0, 0), (0, 0), (0, 0), (Q, Lp - Ls + Q), (0, 0))
    ks = jnp.pad(strided(k), pad_kv).reshape(B, d, H, nb + 2, Q, Dh)
    vs = jnp.pad(strided(v), pad_kv).reshape(B, d, H, nb + 2, Q, Dh)
    qb = qs.reshape(B, d, H, nb, Q, Dh)
    kb = jnp.concatenate([ks[:, :, :, :-2], ks[:, :, :, 1:-1], ks[:, :, :, 2:]], axis=4)
    vb = jnp.concatenate([vs[:, :, :, :-2], vs[:, :, :, 1:-1], vs[:, :, :, 2:]], axis=4)

    qi = jnp.arange(Q)[:, None]
    kj = jnp.arange(3 * Q)[None, :]
    rel = kj - Q - qi
    band = jnp.abs(rel) <= n
    j_abs = (jnp.arange(nb)[:, None] - 1) * Q + jnp.arange(3 * Q)[None, :]
    valid = (j_abs >= 0) & (j_abs < Ls)
    mask = band[None] & valid[:, None, :]
    bias = jnp.moveaxis(rel_bias.astype(jnp.float32)[t5_bucket(rel * d)], -1, 0)

    s = jnp.einsum('brhnqc,brhnkc->brhnqk', qb, kb).astype(jnp.float32) * (Dh ** -0.5)
    s = jnp.where(mask, s + bias[:, None], NEG)
    m = jnp.max(s, axis=-1, keepdims=True)
    e = jnp.exp(s - m)
    l = jnp.sum(e, axis=-1)
    o = jnp.einsum('brhnqk,brhnkc->brhnqc', e.astype(vb.dtype), vb).astype(jnp.float32) / l[..., None]
    lse = m[..., 0] + jnp.log(l)

    o = o.reshape(B, d, H, Lp, Dh)[:, :, :, :Ls].transpose(0, 3, 1, 2, 4).reshape(B, S, H, Dh)
    lse = lse.reshape(B, d, H, Lp)[:, :, :, :Ls].transpose(0, 3, 1, 2).reshape(B, S, H)
    return o, lse


def hyena_filter(L, w1, b1, wi, bi, wo, freq, decay):
    f32 = jnp.float32
    pos = jnp.arange(L, dtype=f32)
    t = pos / max(L - 1, 1)
    bands = (HY_EMB - 1) // 2
    fr = jnp.linspace(1e-4, bands - 1, bands, dtype=f32)
    ang = (2.0 * math.pi / L) * pos[:, None] * fr[None, :]
    z = jnp.concatenate([t[:, None], jnp.cos(ang), -jnp.sin(ang)], axis=-1)
    fq = freq.astype(f32)
    hdn = jnp.sin(fq * (z @ w1.astype(f32) + b1.astype(f32)))
    for j in range(HY_INNER):
        hdn = jnp.sin(fq * (hdn @ wi[j].astype(f32) + bi[j].astype(f32)))
    filt = hdn @ wo.astype(f32)
    offs = jnp.abs(pos - (L // 2)) / (L / 2)
    return filt * jnp.exp(-offs[:, None] * jnp.abs(decay.astype(f32))[None, :])


def centred_long_conv(v, filt):
    L = v.shape[1]
    vf = jnp.fft.rfft(v.astype(jnp.float32), n=2 * L, axis=1)
    hf = jnp.fft.rfft(filt, n=2 * L, axis=0)
    y = jnp.fft.irfft(vf * hf[None], n=2 * L, axis=1)
    return y[:, L // 2: L // 2 + L]


def short_conv(u, w, b):
    C = u.shape[-1]
    y = lax.conv_general_dilated(u, w[:, None, :].astype(u.dtype), window_strides=(1,),
                                 padding=[(1, 1)], dimension_numbers=('NWC', 'WIO', 'NWC'),
                                 feature_group_count=C)
    return y + b.astype(u.dtype)


def setup_inputs(seed: int = 0) -> dict:
    key = jax.random.key(seed)
    ks = jax.random.split(key, 32)
    f32 = jnp.float32
    nrm = lambda k, shape, scale: jax.random.normal(k, shape, f32) * scale
    gain = lambda k, shape: 1.0 + 0.05 * jax.random.normal(k, shape, f32)
    L = DEPTH
    return {
        "x": jax.random.normal(ks[0], (BATCH, SEQ, D_MODEL), f32),
        "p": jax.random.normal(ks[1], (DEPTH, BATCH, SEQ, PLE_DIM), f32),
        "rel_bias": nrm(ks[2], (N_BUCKETS, ATTN_HEADS), 0.5),
        "norm1": gain(ks[3], (L, D_MODEL)),
        "w_in": nrm(ks[4], (L, D_MODEL, IN_W), D_MODEL ** -0.5),
        "q_norm": gain(ks[5], (L, HEAD_DIM)),
        "k_norm": gain(ks[6], (L, HEAD_DIM)),
        "conv_w": nrm(ks[7], (L, HY_SHORT_CONV, 3 * HY_W), HY_SHORT_CONV ** -0.5),
        "conv_b": nrm(ks[8], (L, 3 * HY_W), 0.02),
        "hy_w1": nrm(ks[9], (L, HY_EMB, HY_FILTER_WIDTH), HY_EMB ** -0.5),
        "hy_b1": nrm(ks[10], (L, HY_FILTER_WIDTH), 0.1),
        "hy_wi": nrm(ks[11], (L, HY_INNER, HY_FILTER_WIDTH, HY_FILTER_WIDTH), HY_FILTER_WIDTH ** -0.5),
        "hy_bi": nrm(ks[12], (L, HY_INNER, HY_FILTER_WIDTH), 0.1),
        "hy_wo": nrm(ks[13], (L, HY_FILTER_WIDTH, HY_W), HY_FILTER_WIDTH ** -0.5),
        "hy_freq": gain(ks[14], (L, HY_FILTER_WIDTH)),
        "hy_decay": jnp.exp(jax.random.uniform(ks[15], (L, HY_W), f32,
                                                math.log(HY_DECAY_MIN), math.log(HY_DECAY_MAX))),
        "hy_bias": nrm(ks[16], (L, HY_W), 1.0),
        "attn_out_norm": gain(ks[17], (L, ATTN_W)),
        "hy_out_norm": gain(ks[18], (L, HY_W)),
        "w_out": nrm(ks[19], (L, ATTN_W + HY_W, D_MODEL), (ATTN_W + HY_W) ** -0.5),
        "norm2": gain(ks[20], (L, D_MODEL)),
        "w_gu": nrm(ks[21], (L, D_MODEL, 2 * FFN_HIDDEN), D_MODEL ** -0.5),
        "w_down": nrm(ks[22], (L, FFN_HIDDEN, D_MODEL), FFN_HIDDEN ** -0.5),
        "ple_norm": gain(ks[23], (L, D_MODEL)),
        "w_ple_gate": nrm(ks[24], (L, D_MODEL, D_MODEL), D_MODEL ** -0.5),
        "w_ple_proj": nrm(ks[25], (L, PLE_DIM, D_MODEL), PLE_DIM ** -0.5),
        "ple_post_norm": gain(ks[26], (L, D_MODEL)),
    }


def reference(x, p, rel_bias, norm1, w_in, q_norm, k_norm, conv_w, conv_b, hy_w1, hy_b1,
              hy_wi, hy_bi, hy_wo, hy_freq, hy_decay, hy_bias, attn_out_norm, hy_out_norm,
              w_out, norm2, w_gu, w_down, ple_norm, w_ple_gate, w_ple_proj, ple_post_norm):
    B, S, _ = x.shape
    h = x
    for i in range(DEPTH):
        u = rms_norm(h, norm1[i]) @ w_in[i]
        q, k, v, hy = jnp.split(u, [ATTN_W, 2 * ATTN_W, 3 * ATTN_W], axis=-1)

        q = rms_norm(q.reshape(B, S, ATTN_HEADS, HEAD_DIM), q_norm[i])
        k = rms_norm(k.reshape(B, S, ATTN_HEADS, HEAD_DIM), k_norm[i])
        v = v.reshape(B, S, ATTN_HEADS, HEAD_DIM)
        outs, lses = [], []
        for window, dilation in ATTN_PATTERNS:
            o_g, lse_g = dilated_window_attention(q, k, v, rel_bias, window, dilation)
            outs.append(o_g)
            lses.append(lse_g)
        wts = jax.nn.softmax(jnp.stack(lses), axis=0)
        y_att = jnp.einsum('gbsh,gbshc->bshc', wts, jnp.stack(outs))
        y_att = y_att.reshape(B, S, ATTN_W).astype(h.dtype)

        hy = short_conv(hy, conv_w[i], conv_b[i])
        x0, x1, hv = jnp.split(hy, 3, axis=-1)
        filt = hyena_filter(S, hy_w1[i], hy_b1[i], hy_wi[i], hy_bi[i], hy_wo[i], hy_freq[i], hy_decay[i])
        z = (hv * x1).astype(jnp.float32)
        z = centred_long_conv(z, filt) + z * hy_bias[i].astype(jnp.float32)
        y_hy = (z * x0.astype(jnp.float32)).astype(h.dtype)

        y = jnp.concatenate([group_rms_norm(y_att, attn_out_norm[i], ATTN_HEADS),
                             group_rms_norm(y_hy, hy_out_norm[i], HY_GROUPS)], axis=-1)
        h = h + y @ w_out[i]

        a, g = jnp.split(rms_norm(h, norm2[i]) @ w_gu[i], 2, axis=-1)
        h = h + (jax.nn.silu(a) * g) @ w_down[i]

        gate = jax.nn.sigmoid(rms_norm(h, ple_norm[i]) @ w_ple_gate[i])
        e = rms_norm(p[i] @ w_ple_proj[i], ple_post_norm[i])
        h = h + gate * e
    return h
```

```python
import math
import numpy as np
import concourse.bass as bass
import concourse.mybir as mybir
from concourse.bass_utils import run_bass_kernel_spmd

F32 = mybir.dt.float32
BF16 = mybir.dt.bfloat16
AF = mybir.ActivationFunctionType
ALU = mybir.AluOpType

S = 4096
D = 2048
NH = 8
FFN = 5632
EPS = 1e-6
ENGS = ("pe", "act", "dve", "pool", "sp")
EPOCH = 2000
NDMASEM = 12


class Op:
    __slots__ = ("eng", "emit", "deps", "dma", "milestone", "ms_idx", "sem", "semval", "final", "prev", "grp")

    def __init__(self, eng, emit, dma):
        self.eng = eng
        self.emit = emit
        self.dma = dma
        self.deps = set()
        self.milestone = False
        self.ms_idx = None
        self.sem = None
        self.semval = None
        self.final = False
        self.prev = 0
        self.grp = 0


class Prog:
    def __init__(self, nc):
        self.nc = nc
        self.ops = []
        self.last_w = {}
        self.readers = {}
        self.barrier_deps = []

    def add(self, eng, emit, reads=(), writes=(), dma=False, final=False, deps=(), grp=0):
        op = Op(eng, emit, dma)
        op.final = final
        op.grp = grp
        for d_ in deps:
            if d_ is not None:
                op.deps.add(d_)
        for t in reads:
            w = self.last_w.get(t)
            if w is not None:
                op.deps.add(w)
        for t in writes:
            w = self.last_w.get(t)
            if w is not None:
                op.deps.add(w)
            for r in self.readers.get(t, ()):
                op.deps.add(r)
        for t in reads:
            self.readers.setdefault(t, []).append(op)
        for t in writes:
            self.last_w[t] = op
            self.readers[t] = []
        for d in self.barrier_deps:
            op.deps.add(d)
        op.deps.discard(op)
        self.ops.append(op)
        return op

    def barrier(self):
        last = {}
        dmas = {}
        for o in self.ops:
            if o.dma:
                if o.grp == 0:
                    dmas.setdefault(o.eng, []).append(o)
            else:
                last[o.eng] = o
        deps = list(last.values())
        for e, lst in dmas.items():
            deps.extend(lst[-NDMASEM:])
        self.barrier_deps = deps

    def build(self):
        nc = self.nc
        ops = self.ops
        for op in ops:
            for d in op.deps:
                if d.dma:
                    continue
                if d.eng == "pe" and op.eng == "pe" and not op.dma:
                    continue
                d.milestone = True
        per_eng = {e: [o for o in ops if o.eng == e] for e in ENGS}
        n_ms = {}
        for e in ENGS:
            k = 0
            for o in per_eng[e]:
                if o.milestone and not o.dma:
                    o.ms_idx = k
                    k += 1
            n_ms[e] = k
        sems = {}
        for e in ENGS:
            for ep in range((n_ms[e] + EPOCH - 1) // EPOCH):
                sems[(e, ep)] = nc.alloc_semaphore(f"s_{e}_{ep}")
        dsems = {}
        NCAST = 6
        for e in ENGS:
            if any(o.dma for o in per_eng[e]):
                dsems[e] = [nc.alloc_semaphore(f"d_{e}_{i}") for i in range(NDMASEM + NCAST)]
        for e in dsems:
            tot = [0] * (NDMASEM + NCAST)
            k = [0, 0]
            for o in per_eng[e]:
                if o.dma:
                    if o.grp == 0:
                        i = k[0] % NDMASEM
                    else:
                        i = NDMASEM + k[1] % NCAST
                    k[o.grp] += 1
                    o.sem = (e, i)
                    o.prev = tot[i]
                    tot[i] += 16
                    o.semval = tot[i]
        finals = [o for o in ops if o.final]
        handles = {"pe": "tensor", "act": "scalar", "dve": "vector", "pool": "gpsimd", "sp": "sync"}

        def need(o):
            if o.dma:
                return (("D",) + o.sem, o.semval)
            return ((o.eng, o.ms_idx // EPOCH), o.ms_idx % EPOCH + 1)

        def semh(key):
            if key[0] == "D":
                return dsems[key[1]][key[2]]
            return sems[key]

        def run_engine(e, eh):
            known = {}
            for o in per_eng[e]:
                waits = {}
                for d in o.deps:
                    if (not d.dma) and d.eng == "pe" and e == "pe" and not o.dma:
                        continue
                    k, v = need(d)
                    if known.get(k, 0) >= v:
                        continue
                    if waits.get(k, 0) < v:
                        waits[k] = v
                if o.dma and o.prev > 0:
                    k = ("D",) + o.sem
                    if known.get(k, 0) < o.prev and waits.get(k, 0) < o.prev:
                        waits[k] = o.prev
                for k, v in waits.items():
                    eh.wait_ge(semh(k), v)
                    known[k] = v
                inst = o.emit(eh)
                if o.dma:
                    inst.then_inc(dsems[o.sem[0]][o.sem[1]], 16)
                elif o.milestone:
                    inst.then_inc(sems[(e, o.ms_idx // EPOCH)], 1)
            if e == "sp":
                for o in finals:
                    k, v = need(o)
                    if known.get(k, 0) < v:
                        eh.wait_ge(semh(k), v)
                        known[k] = v

        with nc.Block() as block:
            for e in ENGS:
                if not per_eng[e] and e != "sp":
                    continue
                getattr(block, handles[e])(lambda eh, e=e: run_engine(e, eh))
        return {e: len(per_eng[e]) for e in ENGS}


class Arena:
    def __init__(self, nc, nbytes):
        self.t = nc.alloc_sbuf_tensor("arena", [128, nbytes // 2], BF16)
        self.total = nbytes // 2
        self.off = 0

    def alloc(self, nfree, dtype, parts=128):
        nb = nfree * (4 if dtype == F32 else 2)
        nb = (nb + 63) // 64 * 64
        st = self.off
        self.off += nb // 2
        assert self.off <= self.total, f"arena overflow {self.off * 2}"
        v = self.t[0:parts, st:st + nb // 2]
        if dtype == F32:
            v = v.bitcast(F32)
        return v[:, 0:nfree]

    def mark(self):
        return self.off

    def reset(self, m):
        self.off = m

    def remaining(self):
        return (self.total - self.off) * 2

    def sub(self, nbytes):
        c = Arena.__new__(Arena)
        c.t = self.t
        c.off = self.off
        c.total = self.off + nbytes // 2
        assert c.total <= self.total
        self.off = c.total
        return c


VC = {}
_o = 0
for _n, _w in (("norm1", 16), ("norm2", 16), ("ple_norm", 16), ("ple_post", 16), ("attn_on", 8), ("hy_on", 8),
               ("qn", 1), ("kn", 1), ("cw0", 24), ("cw1", 24), ("cw2", 24), ("cb", 24), ("decay", 8), ("hbias", 8),
               ("b1", 1), ("bi0", 1), ("bi1", 1), ("freq", 1)):
    VC[_n] = _o
    _o += _w
NV = _o


def build_program(dbg=False, stop_after=99, y_from_input=False, il_mode="seq"):
    nc = bass.Bass("TRN2", target_bir_lowering=False)
    P = Prog(nc)

    def din(name, shape, dt=F32):
        return nc.dram_tensor(name, list(shape), dt, kind="ExternalInput").ap()

    def scratch(name, shape, dt):
        return nc.dram_tensor(name, list(shape), dt, kind=("ExternalOutput" if dbg else "Internal")).ap()

    xT = din("xT", [D, S])
    xN = din("xN", [32, 128, 16 * 128])
    pT = din("pT", [256, S])
    w_in = din("w_in", [D, 6144])
    w_out = din("w_out", [D, D])
    w_gu = din("w_gu", [D, 2 * FFN])
    w_down = din("w_down", [FFN, D])
    w_pg = din("w_pg", [D, D])
    w_pp = din("w_pp", [256, D])
    vecs_d = din("vecs", [128, NV])
    relb_d = din("rel_bias", [32, 8])
    oh_d = din("c_oh", [32, 3 * 384])
    valid_d = din("c_valid", [8, 3 * 384])
    cneg_d = din("c_neg", [8, 3 * 384])
    sel_d = din("c_sel", [8, 8 * 128])
    hw1_d = din("hy_w1", [33, 64])
    hwi_d = din("hy_wi", [2, 64, 64])
    hwo_d = din("hy_wo", [64, 1024])
    czf_d = din("c_zf", [33, S])
    coffs_d = din("c_offs", [S])
    cfa_d = din("c_fa", [32, 128])
    cfai_d = din("c_fai", [128, 32])
    cb_d = din("c_b", [64, 128, 768])
    outT = nc.dram_tensor("outT", [D, S], F32, kind="ExternalOutput").ap()

    wb_in = scratch("wb_in", [12, D, 512], BF16)
    wb_out = scratch("wb_out", [4, D, 512], BF16)
    wb_gu = scratch("wb_gu", [22, D, 512], BF16)
    wb_down = scratch("wb_down", [8, FFN, 256], BF16)
    wb_pg = scratch("wb_pg", [4, D, 512], BF16)
    wb_pp = scratch("wb_pp", [256, D], BF16)
    qkT_s = scratch("qkT_s", [2048, S], BF16)
    v_s = scratch("v_s", [S, 1024], BF16)
    hyT_s = scratch("hyT_s", [3072, S], BF16)
    if y_from_input:
        yT_s = din("yT_s", [D, S], BF16)
    else:
        yT_s = scratch("yT_s", [D, S], BF16)

    mvec_s = scratch("mvec_s", [24, 128 * 384], BF16)
    cB_s = scratch("cB_s", [64, 128, 768], BF16)
    filt_s = scratch("filt_s", [S, 1024], BF16)
    z_s = scratch("z_s", [S, 1024], BF16)
    zT_s = scratch("zT_s", [1024, S], BF16)
    x0T_s = scratch("x0T_s", [1024, S], BF16)
    A_s = scratch("A_s", [128, 128, 1024], BF16)
    Hf_s = scratch("Hf_s", [64, 128, 2, 1024], BF16)
    G_s = scratch("G_s", [128, 128, 1024], BF16)
    conv_s = scratch("conv_s", [S, 1024], BF16)
    arena = Arena(nc, 206 * 1024)
    ps = [nc.alloc_psum_tensor(f"ps{i}", [128, 512], F32)[:] for i in range(8)]
    PS = [f"ps{i}" for i in range(8)]

    ones_bf = arena.alloc(128, BF16)
    vecs = arena.alloc(NV, F32)
    qk_g = arena.alloc(2, F32)
    P.add("pool", lambda e: e.memset(ones_bf, 1.0), writes=["ones"])
    P.add("sp", lambda e: e.dma_start(out=vecs, in_=vecs_d), writes=["vecs"], dma=True)
    P.add("dve", lambda e: e.tensor_scalar(out=qk_g[:, 0:1], in0=vecs[:, VC["qn"]:VC["qn"] + 1], scalar1=128 ** -0.5, scalar2=None, op0=ALU.mult),
          reads=["vecs"], writes=["qk_g"])
    P.add("dve", lambda e: e.tensor_copy(out=qk_g[:, 1:2], in_=vecs[:, VC["kn"]:VC["kn"] + 1]), reads=["vecs"], writes=["qk_g"])
    base_mark = arena.mark()

    cast_jobs = []

    def cast(dst, src, tok):
        cast_jobs.append((dst, src, tok))

    for og in range(12):
        cast(wb_in[og], w_in[:, og * 512:(og + 1) * 512], f"wb_in{og}")
    for og in range(4):
        cast(wb_out[og], w_out[:, og * 512:(og + 1) * 512], f"wb_out{og}")
    for j in range(22):
        cast(wb_gu[j, :, 0:256], w_gu[:, j * 256:(j + 1) * 256], f"wb_gua{j}")
        cast(wb_gu[j, :, 256:512], w_gu[:, FFN + j * 256:FFN + (j + 1) * 256], f"wb_gug{j}")
    for og in range(8):
        cast(wb_down[og], w_down[:, og * 256:(og + 1) * 256], f"wb_down{og}")
    for og in range(4):
        cast(wb_pg[og], w_pg[:, og * 512:(og + 1) * 512], f"wb_pg{og}")
    cast(wb_pp, w_pp, "wb_pp")

    def emit_casts(n):
        for _ in range(n):
            if not cast_jobs:
                return
            dst, src, tok = cast_jobs.pop(0)
            P.add("pool", lambda e, dst=dst, src=src: e.dma_start(out=dst, in_=src), writes=[tok], dma=True, grp=1)

    emit_casts(12)

    xn = arena.alloc(16 * S, BF16).rearrange("p (k t) -> p k t", k=16)
    TS = 128
    x_sb = [arena.alloc(16 * TS, F32).rearrange("p (k t) -> p k t", k=16) for _ in range(2)]
    sq_sb = [arena.alloc(16 * TS, BF16).rearrange("p (k t) -> p k t", k=16) for _ in range(2)]
    rs_sb = [arena.alloc(TS, F32) for _ in range(2)]
    slab = [arena.alloc(16 * 512, BF16).rearrange("p (k n) -> p k n", k=16) for _ in range(2)]
    stg = [arena.alloc(4 * 512, BF16).rearrange("p (m n) -> p m n", m=4) for _ in range(2)]
    sqq = [arena.alloc(512, BF16) for _ in range(2)]
    rs2 = [arena.alloc(512, F32) for _ in range(2)]
    xTv = xT.rearrange("(k p) t -> p k t", p=128)
    g1 = VC["norm1"]

    def norm_a(tt):
        b = tt % 2
        tsl = slice(tt * TS, (tt + 1) * TS)
        P.add("sp", lambda e: e.dma_start(out=x_sb[b], in_=xN[tt].rearrange("p (k t) -> p k t", k=16)), writes=[f"x_sb{b}"], dma=True)
        P.add("act", lambda e: e.activation(out=sq_sb[b], in_=x_sb[b], func=AF.Square), reads=[f"x_sb{b}"], writes=[f"sq{b}"])
        pb = 6 + b
        for kc in range(16):
            P.add("pe", lambda e, kc=kc: e.matmul(ps[pb][:, 0:TS], lhsT=ones_bf, rhs=sq_sb[b][:, kc, :], start=(kc == 0), stop=(kc == 15)),
                  reads=[f"sq{b}", "ones"], writes=[PS[pb]])

    def norm_b(tt):
        b = tt % 2
        t5 = tt // 4
        tsl = slice(tt * TS, (tt + 1) * TS)
        pb = 6 + b
        P.add("act", lambda e: e.activation(out=rs_sb[b], in_=ps[pb][:, 0:TS], func=AF.Ln, scale=1.0 / D, bias=EPS), writes=[PS[pb], f"rs{b}"])
        P.add("act", lambda e: e.activation(out=rs_sb[b], in_=rs_sb[b], func=AF.Exp, scale=-0.5), writes=[f"rs{b}"])
        for kc in range(16):
            P.add("dve", lambda e, kc=kc: e.scalar_tensor_tensor(out=xn[:, kc, tsl], in0=x_sb[b][:, kc, :], scalar=vecs[:, g1 + kc:g1 + kc + 1],
                                                                 in1=rs_sb[b], op0=ALU.mult, op1=ALU.mult),
                  reads=[f"x_sb{b}", f"rs{b}", "vecs"], writes=[f"xn{t5}_{kc}"])

    qkv = qkT_s.rearrange("(c p) t -> p c t", p=128)
    hyv = hyT_s.rearrange("(c p) t -> p c t", p=128)
    cnt = {"bank": 0, "n": 0, "stg": 0}
    pending = []

    def flush_pending():
        while pending:
            pending.pop(0)()

    def load_slab(og):
        sb = og % 2
        P.add("sp", lambda e: e.dma_start(out=slab[sb], in_=wb_in[og].rearrange("(k p) n -> p k n", p=128)),
              reads=[f"wb_in{og}"], writes=[f"slab{sb}"], dma=True)

    def gemm_tile(og, t):
        sb = og % 2
        tsl = slice(t * 512, (t + 1) * 512)
        si = cnt["stg"] % 2
        cnt["stg"] += 1
        for m in range(4):
            pb = cnt["bank"] % 4
            cnt["bank"] += 1
            for kc in range(16):
                P.add("pe", lambda e, kc=kc, pb=pb, m=m: e.matmul(ps[pb], lhsT=slab[sb][:, kc, m * 128:(m + 1) * 128], rhs=xn[:, kc, tsl],
                                                                 start=(kc == 0), stop=(kc == 15)),
                      reads=[f"slab{sb}", f"xn{t}_{kc}"], writes=[PS[pb]])
            flush_pending()
            if og < 4:
                n = cnt["n"] % 2
                cnt["n"] += 1
                gcol = 0 if og < 2 else 1
                P.add("act", lambda e, pb=pb, n=n: e.activation(out=sqq[n], in_=ps[pb], func=AF.Square), writes=[PS[pb], f"sqq{n}"])

                def post(pb=pb, n=n, si=si, m=m, gcol=gcol):
                    pn = 4 + n
                    P.add("pe", lambda e: e.matmul(ps[pn], lhsT=ones_bf, rhs=sqq[n], start=True, stop=True), reads=[f"sqq{n}", "ones"], writes=[PS[pn]])
                    P.add("act", lambda e: e.activation(out=rs2[n], in_=ps[pn], func=AF.Ln, scale=1.0 / 128, bias=EPS), writes=[PS[pn], f"rs2{n}"])
                    P.add("act", lambda e: e.activation(out=rs2[n], in_=rs2[n], func=AF.Exp, scale=-0.5), writes=[f"rs2{n}"])
                    P.add("dve", lambda e: e.scalar_tensor_tensor(out=stg[si][:, m, :], in0=ps[pb], scalar=qk_g[:, gcol:gcol + 1], in1=rs2[n],
                                                                  op0=ALU.mult, op1=ALU.mult),
                          reads=[f"rs2{n}", "qk_g"], writes=[PS[pb], f"stg{si}"])
                pending.append(post)
            else:
                if m % 2 == 0:
                    P.add("act", lambda e, pb=pb, m=m: e.activation(out=stg[si][:, m, :], in_=ps[pb], func=AF.Copy), writes=[PS[pb], f"stg{si}"])
                else:
                    P.add("dve", lambda e, pb=pb, m=m: e.tensor_copy(out=stg[si][:, m, :], in_=ps[pb]), writes=[PS[pb], f"stg{si}"])
        flush_pending()
        if og < 4:
            dst = qkv[:, og * 4:(og + 1) * 4, tsl]
            tok = [f"qk_s{og * 4 + m}_{t}" for m in range(4)]
        else:
            dst = hyv[:, (og - 6) * 4:(og - 5) * 4, tsl]
            tok = [f"hy_s{(og - 6) * 4 + m}_{t}" for m in range(4)]
        P.add("pool", lambda e: e.dma_start(out=dst, in_=stg[si]), reads=[f"stg{si}"], writes=tok, dma=True)

    load_slab(0)
    load_slab(1)
    norm_a(0)
    for t in range(8):
        for tt in range(4 * t, 4 * t + 4):
            if tt + 1 < 32:
                norm_a(tt + 1)
            norm_b(tt)
        gemm_tile(0, t)
    emit_casts(6)
    for og in range(1, 12):
        sb = og % 2
        if og >= 2:
            load_slab(og)
        if og in (4, 5):
            for ts_ in range(32):
                pb = cnt["bank"] % 4
                cnt["bank"] += 1
                xr = [f"xn{ts_ // 4}_{kc}" for kc in range(16)]
                for kc in range(16):
                    P.add("pe", lambda e, kc=kc, pb=pb, sb=sb, ts_=ts_: e.matmul(ps[pb], lhsT=xn[:, kc, ts_ * 128:(ts_ + 1) * 128], rhs=slab[sb][:, kc, :],
                                                                                 start=(kc == 0), stop=(kc == 15)),
                          reads=[f"slab{sb}", xr[kc]], writes=[PS[pb]])
                flush_pending()
                si = cnt["stg"] % 2
                cnt["stg"] += 1
                eng = "act" if ts_ % 2 == 0 else "dve"
                if eng == "act":
                    P.add("act", lambda e, pb=pb, si=si: e.activation(out=stg[si][:, 0, :], in_=ps[pb], func=AF.Copy), writes=[PS[pb], f"stg{si}"])
                else:
                    P.add("dve", lambda e, pb=pb, si=si: e.tensor_copy(out=stg[si][:, 0, :], in_=ps[pb]), writes=[PS[pb], f"stg{si}"])
                P.add("pool", lambda e, si=si, ts_=ts_, og=og: e.dma_start(out=v_s[ts_ * 128:(ts_ + 1) * 128, (og - 4) * 512:(og - 3) * 512], in_=stg[si][:, 0, :]),
                      reads=[f"stg{si}"], writes=[f"v_s{ts_}_{og}"], dma=True)
            continue
        for t in range(8):
            gemm_tile(og, t)
        emit_casts(6)
    emit_casts(1000)
    if stop_after <= 1:
        return nc, P
    P.barrier()
    arena.reset(base_mark)
    if stop_after >= 2 and not y_from_input:
        LT = 384
        MAGIC = 12582912.0
        PI_LO = 3.1415925
        TWO_PI = 2.0 * math.pi
        ident = arena.alloc(128, BF16)
        P.add("pool", lambda e: e.memset(ident, 0.0), writes=["ident"])
        P.add("pool", lambda e: e.affine_select(out=ident, in_=ident, pattern=[[-1, 128]], compare_op=ALU.not_equal, fill=1.0, base=0, channel_multiplier=1), writes=["ident"])
        if il_mode == "il":
            arA = arena.sub(86 * 1024)
            arH = arena.sub(arena.remaining())
            AT_S, AT_O, AT_L, PD = [0, 1], [2, 2], [3, 3], 1
            HY_BANKS = [4, 5, 6, 7]
        else:
            mk0 = arena.mark()
            arA = arena.sub(arena.remaining())
            arena.reset(mk0)
            arH = arena.sub(arena.remaining())
            AT_S, AT_O, AT_L, PD = [0, 1, 2, 3], [4, 5], [6, 7], 2
            HY_BANKS = list(range(8))

        def attn_thread():
            ar = arA
            masks = [[ar.alloc(256, BF16) for g in range(3)] for h in range(NH)]
            accO = ar.alloc(S, F32)
            accL = ar.alloc(S, F32)
            mk = ar.mark()
            relb_sb = ar.alloc(8, F32, parts=32)
            oh_sb = ar.alloc(3 * LT, F32, parts=32)
            valid_sb = ar.alloc(3 * LT, F32, parts=8)
            cneg_sb = ar.alloc(3 * LT, F32, parts=8)
            sel_sb = ar.alloc(8 * 128, F32, parts=8)
            e_sb = ar.alloc(3 * LT, F32, parts=8)
            mrow = ar.alloc(24 * LT, BF16).rearrange("p (n l) -> p n l", l=LT)
            P.add("sp", lambda e: e.dma_start(out=relb_sb, in_=relb_d), writes=["relb"], dma=True)
            P.add("sp", lambda e: e.dma_start(out=oh_sb, in_=oh_d), writes=["oh"], dma=True)
            P.add("sp", lambda e: e.dma_start(out=valid_sb, in_=valid_d), writes=["valid"], dma=True)
            P.add("sp", lambda e: e.dma_start(out=cneg_sb, in_=cneg_d), writes=["cneg"], dma=True)
            P.add("sp", lambda e: e.dma_start(out=sel_sb, in_=sel_d), writes=["sel"], dma=True)
            for g in range(3):
                pb = g % 2
                P.add("pe", lambda e, g=g, pb=pb: e.matmul(ps[pb][0:8, 0:LT], lhsT=relb_sb, rhs=oh_sb[:, g * LT:(g + 1) * LT], start=True, stop=True),
                      reads=["relb", "oh"], writes=[PS[pb]])
                P.add("dve", lambda e, g=g, pb=pb: e.tensor_tensor(out=e_sb[:, g * LT:(g + 1) * LT], in0=ps[pb][0:8, 0:LT], in1=valid_sb[:, g * LT:(g + 1) * LT], op=ALU.mult),
                      reads=["valid"], writes=[PS[pb], "e_sb"])
            P.add("dve", lambda e: e.tensor_tensor(out=e_sb, in0=e_sb, in1=cneg_sb, op=ALU.add), reads=["cneg"], writes=["e_sb"])
            yield
            for h in range(NH):
                for g in range(3):
                    pb = (h * 3 + g) % 2
                    P.add("pe", lambda e, h=h, g=g, pb=pb: e.matmul(ps[pb][:, 0:LT], lhsT=sel_sb[:, h * 128:(h + 1) * 128], rhs=e_sb[:, g * LT:(g + 1) * LT], start=True, stop=True),
                          reads=["sel", "e_sb"], writes=[PS[pb]])
                    P.add("dve", lambda e, pb=pb, h=h, g=g: e.tensor_copy(out=mrow[:, h * 3 + g, :], in_=ps[pb][:, 0:LT]), writes=[PS[pb], "mrow"])
                yield
            mrow_dma = P.add("sp", lambda e: e.dma_start(out=mvec_s.rearrange("n (p l) -> p n l", l=LT), in_=mrow), reads=["mrow"], writes=["mvec"], dma=True)
            for h in range(NH):
                for g in range(3):
                    P.add("sp", lambda e, h=h, g=g: e.dma_start(out=masks[h][g], in_=bass.AP(mvec_s.tensor, (h * 3 + g) * 128 * LT, [[LT - 1, 128], [1, 256]])),
                          reads=["mvec"], writes=[f"mask{h}_{g}"], dma=True)
            yield
            ar.reset(mk)
            qk_sb = [ar.alloc(2 * S, BF16).rearrange("p (a t) -> p a t", a=2) for _ in range(1)]
            v_sb = [ar.alloc(32 * 128, BF16).rearrange("p (s c) -> p s c", s=32) for _ in range(2)]
            pt_sb = [ar.alloc(256, BF16) for _ in range(4)]
            aysq = [ar.alloc(512, BF16) for _ in range(2)]
            ayrs = [ar.alloc(512, F32) for _ in range(2)]
            aystg = [ar.alloc(512, BF16) for _ in range(2)]
            a2 = {"s": 0, "pt": 0, "v": 0, "grp": 0, "y": 0}
            last_pv = [None, None]
            qkTv = qkT_s.rearrange("(a h p) t -> h p a t", a=2, p=128)
            for h in range(NH):
                qb = 0
                P.add("sp", lambda e, h=h, qb=qb: e.dma_start(out=qk_sb[qb], in_=qkTv[h]),
                      reads=[f"qk_s{c}_{t}" for c in (h, 8 + h) for t in range(8)], writes=[f"qk_sb{qb}"], dma=True, deps=[mrow_dma])
                for g, d in enumerate((1, 4, 16)):
                    Ls = S // d
                    G = min(512, Ls)
                    nkb = Ls // 128
                    vb = a2["v"] % 2
                    a2["v"] += 1
                    for r in range(d):
                        vsrc = bass.AP(v_s.tensor, h * 128 + r * 1024, [[d * 1024, 128], [128 * d * 1024, nkb], [1, 128]])
                        P.add("sp", lambda e, vb=vb, vsrc=vsrc, r=r, d=d: e.dma_start(out=v_sb[vb][:, r::d, :], in_=vsrc),
                              reads=[f"v_s{ts_}_{4 + h // 4}" for ts_ in range(32)], writes=[f"v_sb{vb}_{r}"], dma=True, deps=[last_pv[vb], mrow_dma])
                    for r in range(d):
                        for grp in range(Ls // G):
                            g0 = grp * G
                            po = AT_O[a2["grp"] % 2]
                            pl = AT_L[a2["grp"] % 2]
                            a2["grp"] += 1
                            kbs = [kb for kb in range(g0 // 128 - 1, g0 // 128 + G // 128 + 1) if 0 <= kb < nkb]
                            pend = []
                            first = [True]

                            def pv(kb, pti, qlo, nq, po=po, pl=pl, vb=vb, r=r, d=d, g0=g0, first=first):
                                st = first[0]
                                first[0] = False
                                osl = slice(qlo - g0, qlo - g0 + nq)
                                last_pv[vb] = P.add("pe", lambda e: e.matmul(ps[po][:, osl], lhsT=v_sb[vb][:, kb * d + r, :], rhs=pt_sb[pti][:, 0:nq], start=st, stop=False, skip_group_check=True),
                                                    reads=[f"v_sb{vb}_{r}", f"pt{pti}"], writes=[PS[po]])
                                P.add("pe", lambda e: e.matmul(ps[pl][:, osl], lhsT=ones_bf, rhs=pt_sb[pti][:, 0:nq], start=st, stop=False, skip_group_check=True),
                                      reads=["ones", f"pt{pti}"], writes=[PS[pl]])

                            for kb in kbs:
                                qlo = max(g0, 128 * kb - 64)
                                qhi = min(g0 + G, 128 * kb + 192)
                                nq = qhi - qlo
                                wlo = qlo - (128 * kb - 64)
                                pb = AT_S[a2["s"] % len(AT_S)]
                                a2["s"] += 1
                                pti = a2["pt"] % 4
                                a2["pt"] += 1
                                ksl = slice(r + d * 128 * kb, r + d * (128 * kb + 127) + 1, d)
                                qsl = slice(r + d * qlo, r + d * (qhi - 1) + 1, d)
                                P.add("pe", lambda e, pb=pb, qb=qb, ksl=ksl, qsl=qsl, nq=nq: e.matmul(ps[pb][:, 0:nq], lhsT=qk_sb[qb][:, 1, ksl], rhs=qk_sb[qb][:, 0, qsl], start=True, stop=False),
                                      reads=[f"qk_sb{qb}"], writes=[PS[pb]])
                                P.add("pe", lambda e, pb=pb, nq=nq, wlo=wlo, h=h, g=g: e.matmul(ps[pb][:, 0:nq], lhsT=ident, rhs=masks[h][g][:, wlo:wlo + nq], start=False, stop=True),
                                      reads=[f"mask{h}_{g}", "ident"], writes=[PS[pb]])
                                P.add("act", lambda e, pb=pb, pti=pti, nq=nq: e.activation(out=pt_sb[pti][:, 0:nq], in_=ps[pb][:, 0:nq], func=AF.Exp), writes=[PS[pb], f"pt{pti}"],
                                      deps=[mrow_dma])
                                pend.append((kb, pti, qlo, nq))
                                if len(pend) > PD:
                                    pv(*pend.pop(0))
                            while pend:
                                pv(*pend.pop(0))
                            asl = slice(r + d * g0, r + d * (g0 + G - 1) + 1, d)
                            if g == 0:
                                P.add("act", lambda e, po=po, asl=asl, G=G: e.activation(out=accO[:, asl], in_=ps[po][:, 0:G], func=AF.Copy), writes=[PS[po], "accO"])
                                P.add("dve", lambda e, pl=pl, asl=asl, G=G: e.tensor_copy(out=accL[:, asl], in_=ps[pl][:, 0:G]), writes=[PS[pl], "accL"])
                            else:
                                P.add("dve", lambda e, po=po, asl=asl, G=G: e.tensor_tensor(out=accO[:, asl], in0=ps[po][:, 0:G], in1=accO[:, asl], op=ALU.add), writes=[PS[po], "accO"])
                                P.add("dve", lambda e, pl=pl, asl=asl, G=G: e.tensor_tensor(out=accL[:, asl], in0=ps[pl][:, 0:G], in1=accL[:, asl], op=ALU.add), writes=[PS[pl], "accL"])
                            yield
                P.add("dve", lambda e: e.reciprocal(out=accL, in_=accL), writes=["accL"])
                P.add("pool", lambda e: e.tensor_tensor(out=accO, in0=accO, in1=accL, op=ALU.mult), reads=["accL"], writes=["accO"])
                ga = VC["attn_on"] + h

                def af0(c):
                    yb = c % 2
                    csl = slice(c * 512, (c + 1) * 512)
                    P.add("act", lambda e: e.activation(out=aysq[yb], in_=accO[:, csl], func=AF.Square), reads=["accO"], writes=[f"aysq{yb}"], deps=[mrow_dma])

                def af1(c):
                    yb = c % 2
                    P.add("pe", lambda e: e.matmul(ps[yb], lhsT=ones_bf, rhs=aysq[yb], start=True, stop=True), reads=[f"aysq{yb}", "ones"], writes=[PS[yb]])

                def af2(c):
                    yb = c % 2
                    P.add("act", lambda e: e.activation(out=ayrs[yb], in_=ps[yb], func=AF.Ln, scale=1.0 / 128, bias=EPS), writes=[PS[yb], f"ayrs{yb}"])
                    P.add("act", lambda e: e.activation(out=ayrs[yb], in_=ayrs[yb], func=AF.Exp, scale=-0.5), writes=[f"ayrs{yb}"])

                def af3(c, h=h, ga=ga):
                    yb = c % 2
                    csl = slice(c * 512, (c + 1) * 512)
                    P.add("dve", lambda e: e.scalar_tensor_tensor(out=aystg[yb], in0=accO[:, csl], scalar=vecs[:, ga:ga + 1], in1=ayrs[yb], op0=ALU.mult, op1=ALU.mult),
                          reads=["accO", f"ayrs{yb}", "vecs"], writes=[f"aystg{yb}"])
                    P.add("pool", lambda e: e.dma_start(out=yT_s[h * 128:(h + 1) * 128, csl], in_=aystg[yb]), reads=[f"aystg{yb}"], writes=[f"y_s{h}_{c}"], dma=True)

                yield from pipeline(8, [af0, af1, af2, af3])

        def pipeline(n, stages, lag=1):
            ns = len(stages)
            for step in range(n + (ns - 1) * lag):
                for k, f in enumerate(stages):
                    i = step - k * lag
                    if 0 <= i < n:
                        f(i)
                yield

        def hyena_thread():
            ar = arH
            HBK = HY_BANKS
            NBK = len(HBK)
            fa_f = ar.alloc(128, F32, parts=32)
            fa_bf = ar.alloc(128, BF16, parts=32)
            fai_f = ar.alloc(32, F32)
            fai_bf = ar.alloc(32, BF16)
            P.add("sp", lambda e: e.dma_start(out=fa_f, in_=cfa_d), writes=["fa_f"], dma=True)
            P.add("sp", lambda e: e.dma_start(out=fai_f, in_=cfai_d), writes=["fai_f"], dma=True)
            P.add("dve", lambda e: e.tensor_copy(out=fa_bf, in_=fa_f), reads=["fa_f"], writes=["fa_bf"])
            P.add("dve", lambda e: e.tensor_copy(out=fai_bf, in_=fai_f), reads=["fai_f"], writes=["fai_bf"])
            for i in range(8):
                P.add("pool", lambda e, i=i: e.dma_start(out=cB_s[i * 8:(i + 1) * 8], in_=cb_d[i * 8:(i + 1) * 8]), writes=[f"cB{i}"], dma=True)
            hy_base = ar.mark()
            TMH = 2 if ar.remaining() < 150 * 1024 else 1
            TMC = 1024 // TMH

            def tile_to_tm_pe(tile_ap, tiletok, pb):
                pv_ = ps[pb].bitcast(BF16)
                for q in range(4):
                    P.add("pe", lambda e, pv_=pv_, q=q: e.transpose(pv_[:, q * 128:(q + 1) * 128], tile_ap[:, q * 128:(q + 1) * 128], ident),
                          reads=[tiletok, "ident"], writes=[PS[pb]])

            def tile_to_tm_copy(pb, pc, cl, tm, dtok, par):
                pv_ = ps[pb].bitcast(BF16)
                if par % 2 == 0:
                    P.add("act", lambda e: e.activation(out=tm[:, pc * 4:(pc + 1) * 4, cl * 128:(cl + 1) * 128], in_=pv_[:, 0:512].rearrange("p (q c) -> p q c", q=4), func=AF.Copy),
                          writes=[PS[pb], f"{dtok}tm"])
                else:
                    P.add("dve", lambda e: e.tensor_copy(out=tm[:, pc * 4:(pc + 1) * 4, cl * 128:(cl + 1) * 128], in_=pv_[:, 0:512].rearrange("p (q c) -> p q c", q=4)),
                          writes=[PS[pb], f"{dtok}tm"])

            def store_tm(tm, dst_dram, dtok, chalf):
                for hb_ in range(4):
                    P.add("pool", lambda e, hb_=hb_: e.dma_start(out=dst_dram.rearrange("(b p) c -> p b c", p=128)[:, hb_ * 8:(hb_ + 1) * 8, chalf * TMC:(chalf + 1) * TMC], in_=tm[:, hb_ * 8:(hb_ + 1) * 8, :]),
                          reads=[f"{dtok}tm"], writes=[f"{dtok}{chalf}_{hb_}"], dma=True)

            w1_sb = ar.alloc(64, F32, parts=33)
            wi_sb = ar.alloc(128, F32, parts=64)
            wo_sb = ar.alloc(1024, F32, parts=64)
            fb_sb = ar.alloc(4, F32, parts=64)
            zfc = [ar.alloc(512, F32, parts=33) for _ in range(3)]
            hdn = [ar.alloc(S, F32, parts=64) for _ in range(2)]
            kk = [ar.alloc(512, F32, parts=64) for _ in range(2)]
            offc = [ar.alloc(512, F32) for _ in range(8)]
            ndec = ar.alloc(8, F32)
            NW = 4
            win = [ar.alloc(512, F32) for _ in range(NW)]
            ftile = [ar.alloc(512, BF16) for _ in range(NW)]
            tokmaj = ar.alloc(32 * TMC, BF16).rearrange("p (b c) -> p b c", b=32)
            wo_bf = ar.alloc(1024, BF16, parts=64)
            hfin_bf = ar.alloc(S, BF16, parts=64)
            P.add("sp", lambda e: e.dma_start(out=w1_sb, in_=hw1_d), writes=["w1"], dma=True)
            P.add("sp", lambda e: e.dma_start(out=wi_sb.rearrange("p (j o) -> p j o", j=2), in_=hwi_d.rearrange("j i o -> i j o")), writes=["wi"], dma=True)
            P.add("sp", lambda e: e.dma_start(out=wo_sb, in_=hwo_d), writes=["wo"], dma=True)
            for pc in range(8):
                P.add("sp", lambda e, pc=pc: e.dma_start(out=offc[pc], in_=bass.AP(coffs_d.tensor, pc * 512, [[0, 128], [1, 512]])), writes=[f"offc{pc}"], dma=True)
            fq = VC["freq"]
            for j, nm in enumerate(("b1", "bi0", "bi1")):
                P.add("dve", lambda e, j=j, nm=nm: e.tensor_tensor(out=fb_sb[:, j:j + 1], in0=vecs[0:64, VC[nm]:VC[nm] + 1], in1=vecs[0:64, fq:fq + 1], op=ALU.mult),
                      reads=["vecs"], writes=["fb"])
            P.add("act", lambda e: e.activation(out=ndec, in_=vecs[:, VC["decay"]:VC["decay"] + 8], func=AF.Abs), reads=["vecs"], writes=["ndec"])
            P.add("dve", lambda e: e.tensor_scalar(out=ndec, in0=ndec, scalar1=-1.0, scalar2=None, op0=ALU.mult), writes=["ndec"])
            for layer in range(3):
                dst = hdn[layer % 2]
                src = hdn[(layer + 1) % 2]
                dtok = [f"hdn{layer % 2}_{pc}" for pc in range(8)]
                stok = [f"hdn{(layer + 1) % 2}_{pc}" for pc in range(8)]

                def m1(pc, layer=layer, src=src, stok=stok):
                    psl = slice(pc * 512, (pc + 1) * 512)
                    pb = HBK[pc % NBK]
                    if layer == 0:
                        zb_ = pc % 3
                        P.add("sp", lambda e: e.dma_start(out=zfc[zb_], in_=czf_d[:, psl]), writes=[f"zfc{zb_}"], dma=True)
                        P.add("pe", lambda e: e.matmul(ps[pb][0:64, :], lhsT=w1_sb, rhs=zfc[zb_], start=True, stop=True), reads=["w1", f"zfc{zb_}"], writes=[PS[pb]])
                    else:
                        P.add("pe", lambda e: e.matmul(ps[pb][0:64, :], lhsT=wi_sb[:, (layer - 1) * 64:layer * 64], rhs=src[:, psl], start=True, stop=True),
                              reads=["wi", stok[pc]], writes=[PS[pb]])

                def m2(pc, layer=layer, dst=dst, dtok=dtok):
                    psl = slice(pc * 512, (pc + 1) * 512)
                    pb = HBK[pc % NBK]
                    P.add("act", lambda e: e.activation(out=dst[:, psl], in_=ps[pb][0:64, :], func=AF.Identity, scale=vecs[0:64, fq:fq + 1], bias=fb_sb[:, layer:layer + 1]),
                          reads=["vecs", "fb"], writes=[PS[pb], dtok[pc]])

                def m3(pc, dst=dst, dtok=dtok):
                    psl = slice(pc * 512, (pc + 1) * 512)
                    k_ = kk[pc % 2]
                    kt = f"kk{pc % 2}"
                    P.add("dve", lambda e: e.tensor_scalar(out=k_, in0=dst[:, psl], scalar1=1.0 / TWO_PI, scalar2=MAGIC, op0=ALU.mult, op1=ALU.add), reads=[dtok[pc]], writes=[kt])
                    P.add("dve", lambda e: e.tensor_scalar(out=k_, in0=k_, scalar1=MAGIC, scalar2=None, op0=ALU.subtract), writes=[kt])
                    P.add("dve", lambda e: e.scalar_tensor_tensor(out=dst[:, psl], in0=k_, scalar=-TWO_PI, in1=dst[:, psl], op0=ALU.mult, op1=ALU.add), reads=[kt], writes=[dtok[pc]])
                    P.add("dve", lambda e: e.tensor_scalar(out=dst[:, psl], in0=dst[:, psl], scalar1=-PI_LO, scalar2=PI_LO, op0=ALU.max, op1=ALU.min), writes=[dtok[pc]])

                def m4(pc, dst=dst, dtok=dtok):
                    psl = slice(pc * 512, (pc + 1) * 512)
                    P.add("act", lambda e: e.activation(out=dst[:, psl], in_=dst[:, psl], func=AF.Sin), writes=[dtok[pc]])

                yield from pipeline(8, [m1, m2, m3, m4])
            hfin = hfin_bf
            P.add("pool", lambda e: e.tensor_copy(out=wo_bf, in_=wo_sb), reads=["wo"], writes=["wo"])
            for pc in range(8):
                P.add("pool" if pc % 2 else "dve", lambda e, pc=pc: e.tensor_copy(out=hfin_bf[:, pc * 512:(pc + 1) * 512], in_=hdn[0][:, pc * 512:(pc + 1) * 512]),
                      reads=[f"hdn0_{pc}"], writes=[f"hfb_{pc}"])
            HF_T = [f"hfb_{pc}" for pc in range(8)]

            for chalf in range(TMH):
                ncl = 8 // TMH

                def units(u, chalf=chalf, ncl=ncl):
                    cl, pc = divmod(u, 8)
                    return chalf * ncl + cl, cl, pc

                def f1(u):
                    c, cl, pc = units(u)
                    pb = HBK[u % (NBK // 2)]
                    psl = slice(pc * 512, (pc + 1) * 512)
                    P.add("pe", lambda e: e.matmul(ps[pb], lhsT=wo_bf[:, c * 128:(c + 1) * 128], rhs=hfin[:, psl], start=True, stop=True),
                          reads=["wo", HF_T[pc]], writes=[PS[pb]])
                    fi = u % NW
                    P.add("act", lambda e: e.activation(out=win[fi], in_=offc[pc], func=AF.Exp, scale=ndec[:, c:c + 1]), reads=[f"offc{pc}", "ndec"], writes=[f"win{fi}"])

                def f2(u):
                    c, cl, pc = units(u)
                    pb = HBK[u % (NBK // 2)]
                    fi = u % NW
                    P.add("dve", lambda e: e.tensor_tensor(out=ftile[fi], in0=ps[pb], in1=win[fi], op=ALU.mult), reads=[f"win{fi}"], writes=[PS[pb], f"ftile{fi}"])

                def f3(u):
                    fi = u % NW
                    tile_to_tm_pe(ftile[fi], f"ftile{fi}", HBK[NBK // 2 + u % (NBK // 2)])

                def f4(u):
                    c, cl, pc = units(u)
                    tile_to_tm_copy(HBK[NBK // 2 + u % (NBK // 2)], pc, cl, tokmaj, "filt_s", u)

                yield from pipeline(8 * ncl, [f1, f2, f3, f4])
                store_tm(tokmaj, filt_s, "filt_s", chalf)
                yield
            FILT_ALL = [f"filt_s{ch}_{hb_}" for ch in range(TMH) for hb_ in range(4)]

            ar.reset(hy_base)
            P.barrier()
            diag = ar.alloc(72 * 128, BF16).rearrange("p (j c) -> p j c", j=72)
            NU = 6
            u_sb = [ar.alloc(S + 2, BF16) for _ in range(NU)]
            NC_ = 4
            ctmp = [ar.alloc(512, F32) for _ in range(NC_)]
            zt_t = [ar.alloc(512, BF16) for _ in range(NC_)]
            x0st = [ar.alloc(512, BF16) for _ in range(NC_)]
            tokmaj2 = ar.alloc(32 * TMC, BF16).rearrange("p (b c) -> p b c", b=32)
            for ch in range(24):
                for j in range(3):
                    col = VC[f"cw{j}"] + ch
                    P.add("dve", lambda e, ch=ch, j=j, col=col: e.tensor_scalar(out=diag[:, ch * 3 + j, :], in0=ident, scalar1=vecs[:, col:col + 1], scalar2=None, op0=ALU.mult),
                          reads=["ident", "vecs"], writes=["diag"])
            for i in range(NU):
                P.add("pool", lambda e, i=i: e.memset(u_sb[i][:, 0:1], 0.0), writes=[f"u{i}"])
                P.add("pool", lambda e, i=i: e.memset(u_sb[i][:, S + 1:S + 2], 0.0), writes=[f"u{i}"])
            yield
            hyTv = hyT_s.rearrange("(c p) t -> c p t", p=128)
            zTv = zT_s.rearrange("(c p) t -> c p t", p=128)
            x0Tv = x0T_s.rearrange("(c p) t -> c p t", p=128)
            bset = NBK // 4

            def bank(role, u):
                return HBK[role * bset + u % bset]

            def ubuf(c, which):
                return (c % 2) * 3 + which

            def b0(u):
                c, pc = divmod(u, 8)
                if pc == 0:
                    for which, ch in enumerate((8 + c, 16 + c, c)):
                        ub = ubuf(c, which)
                        P.add("sp", lambda e, ub=ub, ch=ch: e.dma_start(out=u_sb[ub][:, 1:S + 1], in_=hyTv[ch]), reads=[f"hy_s{ch}_{t}" for t in range(8)], writes=[f"u{ub}"], dma=True)

            def conv_mm(ch, ub, pc, pb):
                for j in range(3):
                    P.add("pe", lambda e, j=j: e.matmul(ps[pb], lhsT=diag[:, ch * 3 + j, :], rhs=u_sb[ub][:, pc * 512 + j:pc * 512 + j + 512], start=(j == 0), stop=(j == 2)),
                          reads=["diag", f"u{ub}"], writes=[PS[pb]])

            def b1(u):
                c, pc = divmod(u, 8)
                conv_mm(8 + c, ubuf(c, 0), pc, bank(0, u))
                conv_mm(16 + c, ubuf(c, 1), pc, bank(1, u))
                conv_mm(c, ubuf(c, 2), pc, bank(2, u))

            def b2(u):
                c, pc = divmod(u, 8)
                ti = u % NC_
                b1c = VC["cb"] + 8 + c
                b0c = VC["cb"] + c
                pa, px = bank(0, u), bank(2, u)
                psl = slice(pc * 512, (pc + 1) * 512)
                P.add("act", lambda e: e.activation(out=ctmp[ti], in_=ps[pa], func=AF.Identity, bias=vecs[:, b1c:b1c + 1]), reads=["vecs"], writes=[PS[pa], f"ctmp{ti}"])
                P.add("act", lambda e: e.activation(out=x0st[ti], in_=ps[px], func=AF.Identity, bias=vecs[:, b0c:b0c + 1]), reads=["vecs"], writes=[PS[px], f"x0st{ti}"])
                P.add("pool", lambda e: e.dma_start(out=x0Tv[c][:, psl], in_=x0st[ti]), reads=[f"x0st{ti}"], writes=[f"x0T_s{c}_{pc}"], dma=True)

            def b3(u):
                c, pc = divmod(u, 8)
                ti = u % NC_
                bvc = VC["cb"] + 16 + c
                pb2 = bank(1, u)
                psl = slice(pc * 512, (pc + 1) * 512)
                P.add("dve", lambda e: e.scalar_tensor_tensor(out=zt_t[ti], in0=ps[pb2], scalar=vecs[:, bvc:bvc + 1], in1=ctmp[ti], op0=ALU.add, op1=ALU.mult),
                      reads=[f"ctmp{ti}", "vecs"], writes=[PS[pb2], f"zt_t{ti}"])
                P.add("pool", lambda e: e.dma_start(out=zTv[c][:, psl], in_=zt_t[ti]), reads=[f"zt_t{ti}"], writes=[f"zT_s{c}_{pc}"], dma=True)

            def b4(u):
                ti = u % NC_
                tile_to_tm_pe(zt_t[ti], f"zt_t{ti}", bank(3, u))

            def b5(u):
                c, pc = divmod(u, 8)
                tile_to_tm_copy(bank(3, u), pc, c % (8 // TMH), tokmaj2, "z_s", u)

            for chalf in range(TMH):
                n_ = 64 // TMH
                off_ = chalf * n_
                yield from pipeline(n_, [lambda i, o=off_: b0(i + o), lambda i, o=off_: b1(i + o), lambda i, o=off_: (b2(i + o), b3(i + o)),
                                         lambda i, o=off_: b4(i + o), lambda i, o=off_: b5(i + o)])
                store_tm(tokmaj2, z_s, "z_s", chalf)
                yield
            Z_ALL = [f"z_s{ch}_{hb_}" for ch in range(TMH) for hb_ in range(4)]

            ar.reset(hy_base)
            P.barrier()
            NB = 2 if il_mode == "il" else 4
            in_sb = [ar.alloc(NB * 1024, BF16, parts=32) for _ in range(2)]
            a_out = [ar.alloc(NB * 1024, BF16) for _ in range(2)]
            cA = [0]

            def step_a(src_dram, srctoks):
                srcv = src_dram.rearrange("(n1 n2) c -> n1 (n2 c)", n1=32)
                for blk in range(128 // NB):
                    bi = cA[0] % 2
                    cA[0] += 1
                    P.add("sp", lambda e, bi=bi, blk=blk: e.dma_start(out=in_sb[bi], in_=srcv[:, blk * NB * 1024:(blk + 1) * NB * 1024]), reads=srctoks, writes=[f"in_sb{bi}"], dma=True)
                    for cc in range(NB * 2):
                        pb = HBK[(blk * NB * 2 + cc) % NBK]
                        csl = slice(cc * 512, (cc + 1) * 512)
                        P.add("pe", lambda e, pb=pb, bi=bi, csl=csl: e.matmul(ps[pb], lhsT=fa_bf, rhs=in_sb[bi][:, csl], start=True, stop=True), reads=["fa_bf", f"in_sb{bi}"], writes=[PS[pb]])
                        if cc % 2 == 0:
                            P.add("act", lambda e, pb=pb, bi=bi, csl=csl: e.activation(out=a_out[bi][:, csl], in_=ps[pb], func=AF.Copy), writes=[PS[pb], f"a_out{bi}"])
                        else:
                            P.add("dve", lambda e, pb=pb, bi=bi, csl=csl: e.tensor_copy(out=a_out[bi][:, csl], in_=ps[pb]), writes=[PS[pb], f"a_out{bi}"])
                    P.add("pool", lambda e, bi=bi, blk=blk: e.dma_start(out=A_s[:, blk * NB:(blk + 1) * NB, :].rearrange("m n c -> m (n c)"), in_=a_out[bi]),
                          reads=[f"a_out{bi}"], writes=[f"A_{blk}"], dma=True)
                    yield

            A_ALL = [f"A_{blk}" for blk in range(128 // NB)]
            yield from step_a(filt_s, FILT_ALL)
            NKB = 4
            ar_sb = [ar.alloc(2 * 1024, BF16).rearrange("p (r c) -> p r c", r=2) for _ in range(NKB)]
            cb_sb = [ar.alloc(768, BF16) for _ in range(NKB)]
            h_t = [ar.alloc(2 * 1024, BF16).rearrange("p (r c) -> p r c", r=2) for _ in range(NKB)]
            NX = 3
            x_t = [ar.alloc(2 * 512, BF16).rearrange("p (r c) -> p r c", r=2) for _ in range(NX)]
            tt_ = [ar.alloc(4 * 512, BF16).rearrange("p (r c) -> p r c", r=4) for _ in range(NX)]
            y_t = [ar.alloc(2 * 512, BF16).rearrange("p (r c) -> p r c", r=2) for _ in range(NX)]
            g_t = [ar.alloc(2 * 1024, BF16).rearrange("p (r c) -> p r c", r=2) for _ in range(NKB)]
            BR, BI, NBI, BRT, BIT, NBIT = [slice(i * 128, (i + 1) * 128) for i in range(6)]
            nxs = max(1, NBK // 4)

            def xbanks(i):
                s_ = i % nxs
                return HBK[2 * s_], HBK[2 * s_ + 1]

            def gbanks(i):
                s_ = i % nxs
                return HBK[NBK // 2 + 2 * s_], HBK[NBK // 2 + 2 * s_ + 1]

            def load_k1(k1, b, with_h):
                P.add("sp", lambda e: e.dma_start(out=ar_sb[b][:, 0, :], in_=A_s[k1]), reads=A_ALL, writes=[f"ar{b}r"], dma=True)
                P.add("sp", lambda e: e.dma_start(out=ar_sb[b][:, 1, :], in_=A_s[64 + k1]), reads=A_ALL, writes=[f"ar{b}i"], dma=True)
                P.add("sp", lambda e: e.dma_start(out=cb_sb[b], in_=cB_s[k1]), reads=[f"cB{k1 // 8}"], writes=[f"cb{b}"], dma=True)
                if with_h:
                    P.add("sp", lambda e: e.dma_start(out=h_t[b], in_=Hf_s[k1]), reads=[f"Hf{k1}"], writes=[f"h_t{b}"], dma=True)

            def fwd_b(i):
                k1, half = divmod(i, 2)
                b = k1 % NKB
                pr, pi_ = xbanks(i)
                hs = slice(half * 512, (half + 1) * 512)
                rd = [f"ar{b}r", f"ar{b}i", f"cb{b}"]
                P.add("pe", lambda e: e.matmul(ps[pr], lhsT=cb_sb[b][:, BR], rhs=ar_sb[b][:, 0, hs], start=True, stop=False), reads=rd, writes=[PS[pr]])
                P.add("pe", lambda e: e.matmul(ps[pr], lhsT=cb_sb[b][:, NBI], rhs=ar_sb[b][:, 1, hs], start=False, stop=True), reads=rd, writes=[PS[pr]])
                P.add("pe", lambda e: e.matmul(ps[pi_], lhsT=cb_sb[b][:, BR], rhs=ar_sb[b][:, 1, hs], start=True, stop=False), reads=rd, writes=[PS[pi_]])
                P.add("pe", lambda e: e.matmul(ps[pi_], lhsT=cb_sb[b][:, BI], rhs=ar_sb[b][:, 0, hs], start=False, stop=True), reads=rd, writes=[PS[pi_]])

            def l0(i):
                k1, half = divmod(i, 2)
                if half == 0:
                    load_k1(k1, k1 % NKB, False)

            def l2(i):
                k1, half = divmod(i, 2)
                b = k1 % NKB
                pr, pi_ = xbanks(i)
                hs = slice(half * 512, (half + 1) * 512)
                P.add("act", lambda e: e.activation(out=h_t[b][:, 0, hs], in_=ps[pr], func=AF.Copy), writes=[PS[pr], f"h_t{b}"])
                P.add("dve", lambda e: e.tensor_copy(out=h_t[b][:, 1, hs], in_=ps[pi_]), writes=[PS[pi_], f"h_t{b}"])
                if half == 1:
                    P.add("pool", lambda e: e.dma_start(out=Hf_s[k1], in_=h_t[b]), reads=[f"h_t{b}"], writes=[f"Hf{k1}"], dma=True)

            yield from pipeline(128, [l0, fwd_b, l2])
            yield from step_a(z_s, Z_ALL)

            def z0(i):
                k1, half = divmod(i, 2)
                if half == 0:
                    load_k1(k1, k1 % NKB, True)

            def z2(i):
                pr, pi_ = xbanks(i)
                s_ = i % NX
                P.add("act", lambda e: e.activation(out=x_t[s_][:, 0, :], in_=ps[pr], func=AF.Copy), writes=[PS[pr], f"x_t{s_}r"])
                P.add("act", lambda e: e.activation(out=x_t[s_][:, 1, :], in_=ps[pi_], func=AF.Copy), writes=[PS[pi_], f"x_t{s_}i"])

            def z3(i):
                k1, half = divmod(i, 2)
                b = k1 % NKB
                s_ = i % NX
                hs = slice(half * 512, (half + 1) * 512)
                P.add("dve", lambda e: e.tensor_tensor(out=tt_[s_][:, 0, :], in0=x_t[s_][:, 0, :], in1=h_t[b][:, 0, hs], op=ALU.mult), reads=[f"x_t{s_}r", f"h_t{b}"], writes=[f"tt{s_}a"])
                P.add("dve", lambda e: e.tensor_tensor(out=tt_[s_][:, 1, :], in0=x_t[s_][:, 1, :], in1=h_t[b][:, 1, hs], op=ALU.mult), reads=[f"x_t{s_}i", f"h_t{b}"], writes=[f"tt{s_}b"])
                P.add("dve", lambda e: e.tensor_tensor(out=tt_[s_][:, 2, :], in0=x_t[s_][:, 0, :], in1=h_t[b][:, 1, hs], op=ALU.mult), reads=[f"x_t{s_}r", f"h_t{b}"], writes=[f"tt{s_}c"])
                P.add("dve", lambda e: e.tensor_tensor(out=tt_[s_][:, 3, :], in0=x_t[s_][:, 1, :], in1=h_t[b][:, 0, hs], op=ALU.mult), reads=[f"x_t{s_}i", f"h_t{b}"], writes=[f"tt{s_}d"])

            def z4(i):
                s_ = i % NX
                P.add("dve", lambda e: e.tensor_tensor(out=y_t[s_][:, 0, :], in0=tt_[s_][:, 0, :], in1=tt_[s_][:, 1, :], op=ALU.subtract), reads=[f"tt{s_}a", f"tt{s_}b"], writes=[f"y_t{s_}r"])
                P.add("dve", lambda e: e.tensor_tensor(out=y_t[s_][:, 1, :], in0=tt_[s_][:, 2, :], in1=tt_[s_][:, 3, :], op=ALU.add), reads=[f"tt{s_}c", f"tt{s_}d"], writes=[f"y_t{s_}i"])

            def z5(i):
                k1, half = divmod(i, 2)
                b = k1 % NKB
                s_ = i % NX
                pr, pi_ = gbanks(i)
                rd = [f"y_t{s_}r", f"y_t{s_}i", f"cb{b}"]
                P.add("pe", lambda e: e.matmul(ps[pr], lhsT=cb_sb[b][:, BRT], rhs=y_t[s_][:, 0, :], start=True, stop=False), reads=rd, writes=[PS[pr]])
                P.add("pe", lambda e: e.matmul(ps[pr], lhsT=cb_sb[b][:, BIT], rhs=y_t[s_][:, 1, :], start=False, stop=True), reads=rd, writes=[PS[pr]])
                P.add("pe", lambda e: e.matmul(ps[pi_], lhsT=cb_sb[b][:, BRT], rhs=y_t[s_][:, 1, :], start=True, stop=False), reads=rd, writes=[PS[pi_]])
                P.add("pe", lambda e: e.matmul(ps[pi_], lhsT=cb_sb[b][:, NBIT], rhs=y_t[s_][:, 0, :], start=False, stop=True), reads=rd, writes=[PS[pi_]])

            def z6(i):
                k1, half = divmod(i, 2)
                b = k1 % NKB
                pr, pi_ = gbanks(i)
                hs = slice(half * 512, (half + 1) * 512)
                P.add("act", lambda e: e.activation(out=g_t[b][:, 0, hs], in_=ps[pr], func=AF.Copy), writes=[PS[pr], f"g_t{b}r"])
                P.add("act", lambda e: e.activation(out=g_t[b][:, 1, hs], in_=ps[pi_], func=AF.Copy), writes=[PS[pi_], f"g_t{b}i"])
                if half == 1:
                    P.add("pool", lambda e: e.dma_start(out=G_s[k1], in_=g_t[b][:, 0, :]), reads=[f"g_t{b}r"], writes=[f"G{k1}"], dma=True)
                    P.add("pool", lambda e: e.dma_start(out=G_s[64 + k1], in_=g_t[b][:, 1, :]), reads=[f"g_t{b}i"], writes=[f"G{64 + k1}"], dma=True)

            yield from pipeline(128, [z0, fwd_b, z2, z3, z4, z5, z6])
            G_ALL = [f"G{m}" for m in range(128)]
            convv = conv_s.rearrange("(n1 n2) c -> n1 (n2 c)", n1=32)
            for blk in range(128 // NB):
                bi = cA[0] % 2
                cA[0] += 1
                P.add("sp", lambda e, bi=bi, blk=blk: e.dma_start(out=a_out[bi], in_=G_s[:, blk * NB:(blk + 1) * NB, :].rearrange("m n c -> m (n c)")), reads=G_ALL, writes=[f"a_out{bi}"], dma=True)
                for cc in range(NB * 2):
                    pb = HBK[(blk * NB * 2 + cc) % NBK]
                    csl = slice(cc * 512, (cc + 1) * 512)
                    P.add("pe", lambda e, pb=pb, bi=bi, csl=csl: e.matmul(ps[pb][0:32, :], lhsT=fai_bf, rhs=a_out[bi][:, csl], start=True, stop=True), reads=["fai_bf", f"a_out{bi}"], writes=[PS[pb]])
                    if cc % 2 == 0:
                        P.add("act", lambda e, pb=pb, bi=bi, csl=csl: e.activation(out=in_sb[bi][:, csl], in_=ps[pb][0:32, :], func=AF.Copy), writes=[PS[pb], f"in_sb{bi}"])
                    else:
                        P.add("dve", lambda e, pb=pb, bi=bi, csl=csl: e.tensor_copy(out=in_sb[bi][:, csl], in_=ps[pb][0:32, :]), writes=[PS[pb], f"in_sb{bi}"])
                P.add("pool", lambda e, bi=bi, blk=blk: e.dma_start(out=convv[:, blk * NB * 1024:(blk + 1) * NB * 1024], in_=in_sb[bi]), reads=[f"in_sb{bi}"], writes=[f"conv_{blk}"], dma=True)
                yield

            CONV_ALL = [f"conv_{blk}" for blk in range(128 // NB)]
            ar.reset(hy_base)
            P.barrier()
            ctile = [ar.alloc(4 * 1024, BF16).rearrange("p (q c) -> p q c", q=4) for _ in range(2)]
            ND = 8
            zt_d = [ar.alloc(512, BF16) for _ in range(ND)]
            x0_d = [ar.alloc(512, BF16) for _ in range(ND)]
            yh = [ar.alloc(512, F32) for _ in range(ND)]
            dsq = [ar.alloc(512, BF16) for _ in range(ND)]
            drs = [ar.alloc(512, F32) for _ in range(ND)]
            dstg = [ar.alloc(512, BF16) for _ in range(ND)]
            convt = conv_s.rearrange("(g q p) c -> g p q c", q=4, p=128)
            hbs = NBK // 2

            def d0(u):
                tg, c = divmod(u, 8)
                i = u % ND
                tsl = slice(tg * 512, (tg + 1) * 512)
                if c == 0:
                    P.add("sp", lambda e: e.dma_start(out=ctile[tg % 2], in_=convt[tg]), reads=CONV_ALL, writes=[f"ctile{tg % 2}"], dma=True)
                P.add("sp", lambda e: e.dma_start(out=zt_d[i], in_=zTv[c][:, tsl]), reads=[f"zT_s{c}_{tg}"], writes=[f"zt_d{i}"], dma=True)
                P.add("sp", lambda e: e.dma_start(out=x0_d[i], in_=x0Tv[c][:, tsl]), reads=[f"x0T_s{c}_{tg}"], writes=[f"x0_d{i}"], dma=True)

            def d1(u):
                tg, c = divmod(u, 8)
                pb = HBK[u % hbs]
                pv_ = ps[pb].bitcast(BF16)
                for q in range(4):
                    P.add("pe", lambda e, q=q: e.transpose(pv_[:, q * 128:(q + 1) * 128], ctile[tg % 2][:, q, c * 128:(c + 1) * 128], ident),
                          reads=[f"ctile{tg % 2}", "ident"], writes=[PS[pb]])

            def d2(u):
                tg, c = divmod(u, 8)
                i = u % ND
                pb = HBK[u % hbs]
                pv_ = ps[pb].bitcast(BF16)
                hb_ = VC["hbias"] + c
                P.add("dve", lambda e: e.scalar_tensor_tensor(out=yh[i], in0=zt_d[i], scalar=vecs[:, hb_:hb_ + 1], in1=pv_[:, 0:512], op0=ALU.mult, op1=ALU.add),
                      reads=[f"zt_d{i}", "vecs"], writes=[PS[pb], f"yh{i}"])

            def d3(u):
                i = u % ND
                P.add("pool", lambda e: e.tensor_tensor(out=yh[i], in0=yh[i], in1=x0_d[i], op=ALU.mult), reads=[f"x0_d{i}"], writes=[f"yh{i}"])

            def d4(u):
                i = u % ND
                P.add("act", lambda e: e.activation(out=dsq[i], in_=yh[i], func=AF.Square), reads=[f"yh{i}"], writes=[f"dsq{i}"])

            def d5(u):
                i = u % ND
                pb2 = HBK[hbs + u % hbs]
                P.add("pe", lambda e: e.matmul(ps[pb2], lhsT=ones_bf, rhs=dsq[i], start=True, stop=True), reads=[f"dsq{i}", "ones"], writes=[PS[pb2]])

            def d6(u):
                i = u % ND
                pb2 = HBK[hbs + u % hbs]
                P.add("act", lambda e: e.activation(out=drs[i], in_=ps[pb2], func=AF.Ln, scale=1.0 / 128, bias=EPS), writes=[PS[pb2], f"drs{i}"])
                P.add("act", lambda e: e.activation(out=drs[i], in_=drs[i], func=AF.Exp, scale=-0.5), writes=[f"drs{i}"])

            def d7(u):
                tg, c = divmod(u, 8)
                i = u % ND
                go = VC["hy_on"] + c
                tsl = slice(tg * 512, (tg + 1) * 512)
                P.add("dve", lambda e: e.scalar_tensor_tensor(out=dstg[i], in0=yh[i], scalar=vecs[:, go:go + 1], in1=drs[i], op0=ALU.mult, op1=ALU.mult),
                      reads=[f"yh{i}", f"drs{i}", "vecs"], writes=[f"dstg{i}"])
                P.add("pool", lambda e: e.dma_start(out=yT_s[1024 + c * 128:1024 + (c + 1) * 128, tsl], in_=dstg[i]), reads=[f"dstg{i}"], writes=[f"y_s{8 + c}_{tg}"], dma=True)

            yield from pipeline(64, [d0, d1, d2, d3, d4, d5, d6, d7])

        gens = [attn_thread(), hyena_thread()]
        if il_mode == "seq":
            for _ in gens[0]:
                pass
            P.barrier()
            for _ in gens[1]:
                pass
        elif il_mode == "attn":
            for _ in gens[0]:
                pass
        elif il_mode == "hyena":
            for _ in gens[1]:
                pass
        else:
            alive = [True, True]
            burst = (3, 4)
            while any(alive):
                for gi, g_ in enumerate(gens):
                    for _ in range(burst[gi]):
                        if alive[gi]:
                            try:
                                next(g_)
                            except StopIteration:
                                alive[gi] = False
    if stop_after <= 2:
        return nc, P
    P.barrier()
    arena.reset(base_mark)

    T = 512
    wpp_sb = arena.alloc(2 * D, BF16).rearrange("p (k n) -> p k n", k=2)
    h_sb = arena.alloc(16 * T, F32).rearrange("p (k t) -> p k t", k=16)
    yact = arena.alloc(44 * T, BF16)
    y_sb = yact[:, 0:16 * T].rearrange("p (k t) -> p k t", k=16)
    act_sb = yact.rearrange("p (k t) -> p k t", k=44)
    hn = arena.alloc(16 * T, BF16).rearrange("p (k t) -> p k t", k=16)
    p_sb = arena.alloc(2 * T, F32).rearrange("p (k t) -> p k t", k=2)
    p_bf = arena.alloc(2 * T, BF16).rearrange("p (k t) -> p k t", k=2)
    sq3 = [arena.alloc(T, BF16) for _ in range(2)]
    rstd3 = arena.alloc(T, F32)
    rstd_e = arena.alloc(T, F32)
    tmpa = [arena.alloc(T, F32) for _ in range(2)]
    tmpb = [arena.alloc(T, F32) for _ in range(2)]
    NSL = 4
    slabs3 = [arena.alloc(16 * 512, BF16) for _ in range(NSL)]
    st3 = {"slab": 0, "pair": 0, "sq": 0, "tmp": 0}
    PAIRS = [(0, 1), (2, 3), (6, 7)]
    yTv = yT_s.rearrange("(k p) t -> p k t", p=128)
    outv = outT.rearrange("(k p) t -> p k t", p=128)
    pTv = pT.rearrange("(k p) t -> p k t", p=128)
    P.add("sp", lambda e: e.dma_start(out=wpp_sb, in_=wb_pp.rearrange("(k p) n -> p k n", p=128)), reads=["wb_pp"], writes=["wpp_sb"], dma=True)

    def next_slab():
        i = st3["slab"] % NSL
        st3["slab"] += 1
        return i

    def next_pair():
        i = st3["pair"] % len(PAIRS)
        st3["pair"] += 1
        return PAIRS[i]

    def rms_stats(src_chunks, src_toks, dst_rstd, dst_tok, nfeat):
        n = len(src_chunks)
        for i, (ap_, tk) in enumerate(zip(src_chunks, src_toks)):
            b = st3["sq"] % 2
            st3["sq"] += 1
            P.add("act", lambda e, ap_=ap_, b=b: e.activation(out=sq3[b], in_=ap_, func=AF.Square), reads=[tk], writes=[f"sq3{b}"])
            P.add("pe", lambda e, b=b, i=i: e.matmul(ps[4], lhsT=ones_bf, rhs=sq3[b], start=(i == 0), stop=(i == n - 1)),
                  reads=[f"sq3{b}", "ones"], writes=[PS[4]])
        P.add("act", lambda e: e.activation(out=dst_rstd, in_=ps[4], func=AF.Ln, scale=1.0 / nfeat, bias=EPS), writes=[PS[4], dst_tok])
        P.add("act", lambda e: e.activation(out=dst_rstd, in_=dst_rstd, func=AF.Exp, scale=-0.5), writes=[dst_tok])

    pend3 = []

    def flush3():
        while pend3:
            pend3.pop(0)()

    def stat_chunk(c):
        b = st3["sq"] % 2
        st3["sq"] += 1
        P.add("act", lambda e: e.activation(out=sq3[b], in_=h_sb[:, c, :], func=AF.Square), reads=[f"h{c}"], writes=[f"sq3{b}"])
        pend3.append(lambda: P.add("pe", lambda e: e.matmul(ps[4], lhsT=ones_bf, rhs=sq3[b], start=(c == 0), stop=(c == 15)),
                                   reads=[f"sq3{b}", "ones"], writes=[PS[4]]))

    sqe = [arena.alloc(T, BF16) for _ in range(2)]

    def e_unit(c):
        pb = next_pair()[c % 2]
        for k2 in range(2):
            P.add("pe", lambda e, k2=k2: e.matmul(ps[pb], lhsT=wpp_sb[:, k2, c * 128:(c + 1) * 128], rhs=p_bf[:, k2, :], start=(k2 == 0), stop=(k2 == 1)),
                  reads=["wpp_sb", "p_bf"], writes=[PS[pb]])
        b = c % 2
        P.add("act", lambda e: e.activation(out=sqe[b], in_=ps[pb], func=AF.Square), writes=[PS[pb], f"sqe{b}"])
        pend3.append(lambda: P.add("pe", lambda e: e.matmul(ps[5], lhsT=ones_bf, rhs=sqe[b], start=(c == 0), stop=(c == 15)),
                                   reads=[f"sqe{b}", "ones"], writes=[PS[5]]))

    def norm_to_hn(gname, stats_done=False):
        g0 = VC[gname]
        if stats_done:
            flush3()
            P.add("act", lambda e: e.activation(out=rstd3, in_=ps[4], func=AF.Ln, scale=1.0 / D, bias=EPS), writes=[PS[4], "rstd3"])
            P.add("act", lambda e: e.activation(out=rstd3, in_=rstd3, func=AF.Exp, scale=-0.5), writes=["rstd3"])
        else:
            rms_stats([h_sb[:, kc, :] for kc in range(16)], [f"h{kc}" for kc in range(16)], rstd3, "rstd3", D)
        for kc in range(16):
            P.add("dve", lambda e, kc=kc: e.scalar_tensor_tensor(out=hn[:, kc, :], in0=h_sb[:, kc, :], scalar=vecs[:, g0 + kc:g0 + kc + 1], in1=rstd3,
                                                                  op0=ALU.mult, op1=ALU.mult),
                  reads=[f"h{kc}", "rstd3", "vecs"], writes=[f"hn{kc}"])

    HN = [f"hn{kc}" for kc in range(16)]
    for t in range(S // T):
        tsl = slice(t * T, (t + 1) * T)
        P.add("sp", lambda e, tsl=tsl: e.dma_start(out=y_sb, in_=yTv[:, :, tsl]), reads=[f"y_s{c}_{t}" for c in range(16)], writes=["yact"], dma=True)
        P.add("sp", lambda e, tsl=tsl: e.dma_start(out=p_sb, in_=pTv[:, :, tsl]), writes=["p_sb"], dma=True)
        P.add("dve", lambda e: e.tensor_copy(out=p_bf, in_=p_sb), reads=["p_sb"], writes=["p_bf"])
        for og in range(4):
            si = next_slab()
            sl = slabs3[si].rearrange("p (k n) -> p k n", k=16)
            P.add("sp", lambda e, sl=sl, og=og: e.dma_start(out=sl, in_=wb_out[og].rearrange("(k p) n -> p k n", p=128)),
                  reads=[f"wb_out{og}"], writes=[f"sl3{si}"], dma=True)
            if og == 0:
                for kc in range(16):
                    P.add("sp", lambda e, kc=kc, tsl=tsl: e.dma_start(out=h_sb[:, kc, :], in_=xTv[:, kc, tsl]), writes=[f"h{kc}"], dma=True)
            for m in range(4):
                c = og * 4 + m
                pb = next_pair()[m % 2]
                for kc in range(16):
                    P.add("pe", lambda e, sl=sl, kc=kc, m=m, pb=pb: e.matmul(ps[pb], lhsT=sl[:, kc, m * 128:(m + 1) * 128], rhs=y_sb[:, kc, :],
                                                                            start=(kc == 0), stop=(kc == 15)),
                          reads=[f"sl3{si}", "yact"], writes=[PS[pb]])
                flush3()
                P.add("dve", lambda e, c=c, pb=pb: e.tensor_tensor(out=h_sb[:, c, :], in0=ps[pb], in1=h_sb[:, c, :], op=ALU.add), writes=[PS[pb], f"h{c}"])
                stat_chunk(c)
                e_unit(c)
        norm_to_hn("norm2", stats_done=True)
        P.add("act", lambda e: e.activation(out=rstd_e, in_=ps[5], func=AF.Ln, scale=1.0 / D, bias=EPS), writes=[PS[5], "rstd_e"])
        P.add("act", lambda e: e.activation(out=rstd_e, in_=rstd_e, func=AF.Exp, scale=-0.5), writes=["rstd_e"])
        for j in range(22):
            si = next_slab()
            sl = slabs3[si].rearrange("p (k n) -> p k n", k=16)
            P.add("sp", lambda e, sl=sl, j=j: e.dma_start(out=sl, in_=wb_gu[j].rearrange("(k p) n -> p k n", p=128)),
                  reads=[f"wb_gua{j}", f"wb_gug{j}"], writes=[f"sl3{si}"], dma=True)
            for half in range(2):
                c = 2 * j + half
                pa, pg = next_pair()
                for kc in range(16):
                    P.add("pe", lambda e, sl=sl, kc=kc, half=half, pa=pa: e.matmul(ps[pa], lhsT=sl[:, kc, half * 128:(half + 1) * 128], rhs=hn[:, kc, :],
                                                                                  start=(kc == 0), stop=(kc == 15)),
                          reads=[f"sl3{si}", HN[kc]], writes=[PS[pa]])
                for kc in range(16):
                    P.add("pe", lambda e, sl=sl, kc=kc, half=half, pg=pg: e.matmul(ps[pg], lhsT=sl[:, kc, 256 + half * 128:256 + (half + 1) * 128], rhs=hn[:, kc, :],
                                                                                  start=(kc == 0), stop=(kc == 15)),
                          reads=[f"sl3{si}", HN[kc]], writes=[PS[pg]])
                tb = st3["tmp"] % 2
                st3["tmp"] += 1
                P.add("act", lambda e, pa=pa, tb=tb: e.activation(out=tmpa[tb], in_=ps[pa], func=AF.Silu), writes=[PS[pa], f"tmpa{tb}"])
                P.add("dve", lambda e, pg=pg, tb=tb, c=c: e.tensor_tensor(out=act_sb[:, c, :], in0=ps[pg], in1=tmpa[tb], op=ALU.mult),
                      reads=[f"tmpa{tb}"], writes=[PS[pg], "yact" if c < 16 else f"act{c}"])
        for og in range(8):
            pbs = next_pair()
            for kh in range(2):
                si = next_slab()
                sl = slabs3[si][:, 0:22 * 256].rearrange("p (k n) -> p k n", k=22)
                P.add("sp", lambda e, sl=sl, og=og, kh=kh: e.dma_start(out=sl, in_=wb_down[og, kh * 2816:(kh + 1) * 2816, :].rearrange("(k p) n -> p k n", p=128)),
                      reads=[f"wb_down{og}"], writes=[f"sl3{si}"], dma=True)
                for m in range(2):
                    for k2 in range(22):
                        kc = kh * 22 + k2
                        P.add("pe", lambda e, sl=sl, k2=k2, kc=kc, m=m, pb=pbs[m]: e.matmul(ps[pb], lhsT=sl[:, k2, m * 128:(m + 1) * 128], rhs=act_sb[:, kc, :],
                                                                                           start=(kc == 0), stop=(kc == 43)),
                              reads=[f"sl3{si}", "yact" if kc < 16 else f"act{kc}"], writes=[PS[pbs[m]]])
            flush3()
            for m in range(2):
                c = og * 2 + m
                P.add("dve", lambda e, c=c, pb=pbs[m]: e.tensor_tensor(out=h_sb[:, c, :], in0=ps[pb], in1=h_sb[:, c, :], op=ALU.add), writes=[PS[pbs[m]], f"h{c}"])
                stat_chunk(c)
        flush3()
        norm_to_hn("ple_norm", stats_done=True)
        gp = VC["ple_post"]
        for og in range(4):
            si = next_slab()
            sl = slabs3[si].rearrange("p (k n) -> p k n", k=16)
            P.add("sp", lambda e, sl=sl, og=og: e.dma_start(out=sl, in_=wb_pg[og].rearrange("(k p) n -> p k n", p=128)),
                  reads=[f"wb_pg{og}"], writes=[f"sl3{si}"], dma=True)
            for m in range(4):
                c = og * 4 + m
                pgt, pe_ = next_pair()
                for kc in range(16):
                    P.add("pe", lambda e, sl=sl, kc=kc, m=m, pgt=pgt: e.matmul(ps[pgt], lhsT=sl[:, kc, m * 128:(m + 1) * 128], rhs=hn[:, kc, :],
                                                                              start=(kc == 0), stop=(kc == 15)),
                          reads=[f"sl3{si}", HN[kc]], writes=[PS[pgt]])
                for k2 in range(2):
                    P.add("pe", lambda e, c=c, k2=k2, pe_=pe_: e.matmul(ps[pe_], lhsT=wpp_sb[:, k2, c * 128:(c + 1) * 128], rhs=p_bf[:, k2, :], start=(k2 == 0), stop=(k2 == 1)),
                          reads=["wpp_sb", "p_bf"], writes=[PS[pe_]])
                tb = st3["tmp"] % 2
                st3["tmp"] += 1
                P.add("act", lambda e, pgt=pgt, tb=tb: e.activation(out=tmpa[tb], in_=ps[pgt], func=AF.Sigmoid), writes=[PS[pgt], f"tmpa{tb}"])
                P.add("dve", lambda e, pe_=pe_, tb=tb, c=c: e.scalar_tensor_tensor(out=tmpb[tb], in0=ps[pe_], scalar=vecs[:, gp + c:gp + c + 1], in1=rstd_e,
                                                                                  op0=ALU.mult, op1=ALU.mult),
                      reads=["rstd_e", "vecs"], writes=[PS[pe_], f"tmpb{tb}"])
                P.add("pool", lambda e, tb=tb: e.tensor_tensor(out=tmpb[tb], in0=tmpb[tb], in1=tmpa[tb], op=ALU.mult), reads=[f"tmpa{tb}"], writes=[f"tmpb{tb}"])
                P.add("pool", lambda e, tb=tb, c=c: e.tensor_tensor(out=h_sb[:, c, :], in0=h_sb[:, c, :], in1=tmpb[tb], op=ALU.add), reads=[f"tmpb{tb}"], writes=[f"h{c}"])
                P.add("pool", lambda e, c=c, tsl=tsl: e.dma_start(out=outv[:, c, tsl], in_=h_sb[:, c, :]), reads=[f"h{c}"], dma=True, final=True)
    return nc, P


def _colvec(v, ncol):
    return np.ascontiguousarray(np.asarray(v, np.float32).reshape(ncol, -1).T)


def _t5_bucket(rel):
    rel = np.asarray(rel, np.int64)
    n = np.abs(rel)
    with np.errstate(divide="ignore"):
        large = 8 + (np.log(np.maximum(n, 1).astype(np.float32) / np.float32(8)) / np.float32(math.log(1024 / 8)) * np.float32(8)).astype(np.int32)
    large = np.minimum(large, 15)
    return np.where(rel > 0, 16, 0) + np.where(n < 8, n, large)


def attn_consts():
    LT = 384
    oh = np.zeros((32, 3 * LT), np.float32)
    valid = np.zeros((8, 3 * LT), np.float32)
    for gi, d in enumerate((1, 4, 16)):
        u = np.arange(129)
        b = _t5_bucket(d * (64 - u))
        oh[b, gi * LT + u] = 1.0
        valid[:, gi * LT:gi * LT + 129] = 1.0
    sel = np.zeros((8, 8, 128), np.float32)
    for h in range(8):
        sel[h, h, :] = 1.0
    return oh, valid, np.ascontiguousarray(sel.reshape(8, 8 * 128)), ((valid - 1.0) * 60.0).astype(np.float32)


def hyena_consts():
    L = S
    pos = np.arange(L, dtype=np.float64)
    t = pos / (L - 1)
    fr = np.linspace(1e-4, 15, 16)
    ang = (2.0 * math.pi / L) * pos[:, None] * fr[None, :]
    zf = np.concatenate([t[:, None], np.cos(ang), -np.sin(ang)], axis=-1).T.astype(np.float32)
    offs = (np.abs(pos - (L // 2)) / (L / 2)).astype(np.float32)
    n1 = np.arange(32)[:, None]
    k1 = np.arange(64)[None, :]
    th = 2 * math.pi * n1 * k1 / 64.0
    fa = np.concatenate([np.cos(th), -np.sin(th)], axis=1).astype(np.float32)
    n1p = np.arange(16, 48)[None, :]
    k1c = np.arange(64)[:, None]
    th2 = 2 * math.pi * n1p * k1c / 64.0
    fai = (np.concatenate([np.cos(th2), -np.sin(th2)], axis=0) / 8192.0).astype(np.float32)
    cb = np.zeros((64, 128, 768), np.float32)
    n2 = np.arange(128)[:, None].astype(np.float64)
    k2 = np.arange(128)[None, :].astype(np.float64)
    for k in range(64):
        th3 = 2 * math.pi * n2 * (k + 64 * k2) / 8192.0
        br, bi = np.cos(th3), -np.sin(th3)
        cb[k, :, 0:128] = br
        cb[k, :, 128:256] = bi
        cb[k, :, 256:384] = -bi
        cb[k, :, 384:512] = br.T
        cb[k, :, 512:640] = bi.T
        cb[k, :, 640:768] = -bi.T
    return zf, offs, fa, fai, cb


def make_in_maps(inputs):
    g = {k: np.asarray(v) for k, v in inputs.items()}
    c_oh, c_valid, c_sel, c_neg = attn_consts()
    c_zf, c_offs, c_fa, c_fai, c_b = hyena_consts()
    vec = np.zeros((128, NV), np.float32)

    def put(name, arr):
        arr = np.asarray(arr, np.float32)
        vec[:arr.shape[0], VC[name]:VC[name] + arr.shape[1]] = arr

    put("norm1", _colvec(g["norm1"][0], 16))
    put("norm2", _colvec(g["norm2"][0], 16))
    put("ple_norm", _colvec(g["ple_norm"][0], 16))
    put("ple_post", _colvec(g["ple_post_norm"][0], 16))
    put("attn_on", _colvec(g["attn_out_norm"][0], 8))
    put("hy_on", _colvec(g["hy_out_norm"][0], 8))
    put("qn", _colvec(g["q_norm"][0], 1))
    put("kn", _colvec(g["k_norm"][0], 1))
    for j in range(3):
        put(f"cw{j}", _colvec(g["conv_w"][0, j], 24))
    put("cb", _colvec(g["conv_b"][0], 24))
    put("decay", _colvec(g["hy_decay"][0], 8))
    put("hbias", _colvec(g["hy_bias"][0], 8))
    put("b1", _colvec(g["hy_b1"][0], 1))
    put("bi0", _colvec(g["hy_bi"][0, 0], 1))
    put("bi1", _colvec(g["hy_bi"][0, 1], 1))
    put("freq", _colvec(g["hy_freq"][0], 1))
    shared = {
        "w_in": np.ascontiguousarray(g["w_in"][0]), "w_out": np.ascontiguousarray(g["w_out"][0]),
        "w_gu": np.ascontiguousarray(g["w_gu"][0]), "w_down": np.ascontiguousarray(g["w_down"][0]),
        "w_pg": np.ascontiguousarray(g["w_ple_gate"][0]), "w_pp": np.ascontiguousarray(g["w_ple_proj"][0]),
        "vecs": vec, "rel_bias": np.ascontiguousarray(g["rel_bias"], dtype=np.float32),
        "c_oh": c_oh, "c_valid": c_valid, "c_sel": c_sel, "c_neg": c_neg,
        "hy_w1": np.ascontiguousarray(g["hy_w1"][0], dtype=np.float32), "hy_wi": np.ascontiguousarray(g["hy_wi"][0], dtype=np.float32),
        "hy_wo": np.ascontiguousarray(g["hy_wo"][0], dtype=np.float32),
        "c_zf": c_zf, "c_offs": c_offs, "c_fa": c_fa, "c_fai": c_fai, "c_b": c_b,
    }
    maps = []
    for b in range(8):
        m = dict(shared)
        m["xT"] = np.ascontiguousarray(g["x"][b].T)
        m["xN"] = np.ascontiguousarray(g["x"][b].reshape(32, 128, 16, 128).transpose(0, 3, 2, 1).reshape(32, 128, 2048))
        m["pT"] = np.ascontiguousarray(g["p"][0, b].T)
        maps.append(m)
    return maps


def kernel(**inputs):
    nc, P = build_program()
    P.build()
    in_maps = make_in_maps(inputs)
    res = run_bass_kernel_spmd(nc, in_maps, core_ids=list(range(8)))
    out = np.stack([np.ascontiguousarray(r["outT"].T) for r in res.results], axis=0)
    return out.astype(np.float32)
```

```python
import math
import numpy as np
import concourse.bass as bass
import concourse.mybir as mybir
from concourse.bass_utils import run_bass_kernel_spmd

F32 = mybir.dt.float32
BF16 = mybir.dt.bfloat16
AF = mybir.ActivationFunctionType
ALU = mybir.AluOpType

S = 4096
D = 2048
NH = 8
FFN = 5632
EPS = 1e-6
ENGS = ("pe", "act", "dve", "pool", "sp")
EPOCH = 2000
NDMASEM = 12


class Op:
    __slots__ = ("eng", "emit", "deps", "dma", "milestone", "ms_idx", "sem", "semval", "final", "prev", "grp")

    def __init__(self, eng, emit, dma):
        self.eng = eng
        self.emit = emit
        self.dma = dma
        self.deps = set()
        self.milestone = False
        self.ms_idx = None
        self.sem = None
        self.semval = None
        self.final = False
        self.prev = 0
        self.grp = 0


class Prog:
    def __init__(self, nc):
        self.nc = nc
        self.ops = []
        self.last_w = {}
        self.readers = {}
        self.barrier_deps = []

    def add(self, eng, emit, reads=(), writes=(), dma=False, final=False, deps=(), grp=0):
        op = Op(eng, emit, dma)
        op.final = final
        op.grp = grp
        for d_ in deps:
            if d_ is not None:
                op.deps.add(d_)
        for t in reads:
            w = self.last_w.get(t)
            if w is not None:
                op.deps.add(w)
        for t in writes:
            w = self.last_w.get(t)
            if w is not None:
                op.deps.add(w)
            for r in self.readers.get(t, ()):
                op.deps.add(r)
        for t in reads:
            self.readers.setdefault(t, []).append(op)
        for t in writes:
            self.last_w[t] = op
            self.readers[t] = []
        for d in self.barrier_deps:
            op.deps.add(d)
        op.deps.discard(op)
        self.ops.append(op)
        return op

    def barrier(self):
        last = {}
        dmas = {}
        for o in self.ops:
            if o.dma:
                if o.grp == 0:
                    dmas.setdefault(o.eng, []).append(o)
            else:
                last[o.eng] = o
        deps = list(last.values())
        for e, lst in dmas.items():
            deps.extend(lst[-NDMASEM:])
        self.barrier_deps = deps

    def build(self):
        nc = self.nc
        ops = self.ops
        for op in ops:
            for d in op.deps:
                if d.dma:
                    continue
                if d.eng == "pe" and op.eng == "pe" and not op.dma:
                    continue
                d.milestone = True
        per_eng = {e: [o for o in ops if o.eng == e] for e in ENGS}
        n_ms = {}
        for e in ENGS:
            k = 0
            for o in per_eng[e]:
                if o.milestone and not o.dma:
                    o.ms_idx = k
                    k += 1
            n_ms[e] = k
        sems = {}
        for e in ENGS:
            for ep in range((n_ms[e] + EPOCH - 1) // EPOCH):
                sems[(e, ep)] = nc.alloc_semaphore(f"s_{e}_{ep}")
        dsems = {}
        NCAST = 6
        for e in ENGS:
            if any(o.dma for o in per_eng[e]):
                dsems[e] = [nc.alloc_semaphore(f"d_{e}_{i}") for i in range(NDMASEM + NCAST)]
        for e in dsems:
            tot = [0] * (NDMASEM + NCAST)
            k = [0, 0]
            for o in per_eng[e]:
                if o.dma:
                    if o.grp == 0:
                        i = k[0] % NDMASEM
                    else:
                        i = NDMASEM + k[1] % NCAST
                    k[o.grp] += 1
                    o.sem = (e, i)
                    o.prev = tot[i]
                    tot[i] += 16
                    o.semval = tot[i]
        finals = [o for o in ops if o.final]
        handles = {"pe": "tensor", "act": "scalar", "dve": "vector", "pool": "gpsimd", "sp": "sync"}

        def need(o):
            if o.dma:
                return (("D",) + o.sem, o.semval)
            return ((o.eng, o.ms_idx // EPOCH), o.ms_idx % EPOCH + 1)

        def semh(key):
            if key[0] == "D":
                return dsems[key[1]][key[2]]
            return sems[key]

        def run_engine(e, eh):
            known = {}
            for o in per_eng[e]:
                waits = {}
                for d in o.deps:
                    if (not d.dma) and d.eng == "pe" and e == "pe" and not o.dma:
                        continue
                    k, v = need(d)
                    if known.get(k, 0) >= v:
                        continue
                    if waits.get(k, 0) < v:
                        waits[k] = v
                if o.dma and o.prev > 0:
                    k = ("D",) + o.sem
                    if known.get(k, 0) < o.prev and waits.get(k, 0) < o.prev:
                        waits[k] = o.prev
                for k, v in waits.items():
                    eh.wait_ge(semh(k), v)
                    known[k] = v
                inst = o.emit(eh)
                if o.dma:
                    inst.then_inc(dsems[o.sem[0]][o.sem[1]], 16)
                elif o.milestone:
                    inst.then_inc(sems[(e, o.ms_idx // EPOCH)], 1)
            if e == "sp":
                for o in finals:
                    k, v = need(o)
                    if known.get(k, 0) < v:
                        eh.wait_ge(semh(k), v)
                        known[k] = v

        with nc.Block() as block:
            for e in ENGS:
                if not per_eng[e] and e != "sp":
                    continue
                getattr(block, handles[e])(lambda eh, e=e: run_engine(e, eh))
        return {e: len(per_eng[e]) for e in ENGS}


class Arena:
    def __init__(self, nc, nbytes):
        self.t = nc.alloc_sbuf_tensor("arena", [128, nbytes // 2], BF16)
        self.total = nbytes // 2
        self.off = 0

    def alloc(self, nfree, dtype, parts=128):
        nb = nfree * (4 if dtype == F32 else 2)
        nb = (nb + 63) // 64 * 64
        st = self.off
        self.off += nb // 2
        assert self.off <= self.total, f"arena overflow {self.off * 2}"
        v = self.t[0:parts, st:st + nb // 2]
        if dtype == F32:
            v = v.bitcast(F32)
        return v[:, 0:nfree]

    def mark(self):
        return self.off

    def reset(self, m):
        self.off = m

    def remaining(self):
        return (self.total - self.off) * 2

    def sub(self, nbytes):
        c = Arena.__new__(Arena)
        c.t = self.t
        c.off = self.off
        c.total = self.off + nbytes // 2
        assert c.total <= self.total
        self.off = c.total
        return c


VC = {}
_o = 0
for _n, _w in (("norm1", 16), ("norm2", 16), ("ple_norm", 16), ("ple_post", 16), ("attn_on", 8), ("hy_on", 8),
               ("qn", 1), ("kn", 1), ("cw0", 24), ("cw1", 24), ("cw2", 24), ("cb", 24), ("decay", 8), ("hbias", 8),
               ("b1", 1), ("bi0", 1), ("bi1", 1), ("freq", 1)):
    VC[_n] = _o
    _o += _w
NV = _o


def build_program(dbg=False, stop_after=99, y_from_input=False, il_mode="seq"):
    nc = bass.Bass("TRN2", target_bir_lowering=False)
    P = Prog(nc)

    def din(name, shape, dt=F32):
        return nc.dram_tensor(name, list(shape), dt, kind="ExternalInput").ap()

    def scratch(name, shape, dt):
        return nc.dram_tensor(name, list(shape), dt, kind=("ExternalOutput" if dbg else "Internal")).ap()

    xT = din("xT", [D, S])
    xN = din("xN", [32, 128, 16 * 128])
    pT = din("pT", [256, S])
    w_in = din("w_in", [D, 6144])
    w_out = din("w_out", [D, D])
    w_gu = din("w_gu", [D, 2 * FFN])
    w_down = din("w_down", [FFN, D])
    w_pg = din("w_pg", [D, D])
    w_pp = din("w_pp", [256, D])
    vecs_d = din("vecs", [128, NV])
    relb_d = din("rel_bias", [32, 8])
    oh_d = din("c_oh", [32, 3 * 384])
    valid_d = din("c_valid", [8, 3 * 384])
    cneg_d = din("c_neg", [8, 3 * 384])
    sel_d = din("c_sel", [8, 8 * 128])
    hw1_d = din("hy_w1", [33, 64])
    hwi_d = din("hy_wi", [2, 64, 64])
    hwo_d = din("hy_wo", [64, 1024])
    czf_d = din("c_zf", [33, S])
    coffs_d = din("c_offs", [S])
    cfa_d = din("c_fa", [32, 128])
    cfai_d = din("c_fai", [128, 32])
    cb_d = din("c_b", [64, 128, 768])
    outT = nc.dram_tensor("outT", [D, S], F32, kind="ExternalOutput").ap()

    wb_in = scratch("wb_in", [12, D, 512], BF16)
    wb_out = scratch("wb_out", [4, D, 512], BF16)
    wb_gu = scratch("wb_gu", [22, D, 512], BF16)
    wb_down = scratch("wb_down", [8, FFN, 256], BF16)
    wb_pg = scratch("wb_pg", [4, D, 512], BF16)
    wb_pp = scratch("wb_pp", [256, D], BF16)
    qkT_s = scratch("qkT_s", [2048, S], BF16)
    v_s = scratch("v_s", [S, 1024], BF16)
    hyT_s = scratch("hyT_s", [3072, S], BF16)
    if y_from_input:
        yT_s = din("yT_s", [D, S], BF16)
    else:
        yT_s = scratch("yT_s", [D, S], BF16)

    mvec_s = scratch("mvec_s", [24, 128 * 384], BF16)
    cB_s = scratch("cB_s", [64, 128, 768], BF16)
    filt_s = scratch("filt_s", [S, 1024], BF16)
    z_s = scratch("z_s", [S, 1024], BF16)
    zT_s = scratch("zT_s", [1024, S], BF16)
    x0T_s = scratch("x0T_s", [1024, S], BF16)
    A_s = scratch("A_s", [128, 128, 1024], BF16)
    Hf_s = scratch("Hf_s", [64, 128, 2, 1024], BF16)
    G_s = scratch("G_s", [128, 128, 1024], BF16)
    conv_s = scratch("conv_s", [S, 1024], BF16)
    arena = Arena(nc, 206 * 1024)
    ps = [nc.alloc_psum_tensor(f"ps{i}", [128, 512], F32)[:] for i in range(8)]
    PS = [f"ps{i}" for i in range(8)]

    ones_bf = arena.alloc(128, BF16)
    vecs = arena.alloc(NV, F32)
    qk_g = arena.alloc(2, F32)
    P.add("pool", lambda e: e.memset(ones_bf, 1.0), writes=["ones"])
    P.add("sp", lambda e: e.dma_start(out=vecs, in_=vecs_d), writes=["vecs"], dma=True)
    P.add("dve", lambda e: e.tensor_scalar(out=qk_g[:, 0:1], in0=vecs[:, VC["qn"]:VC["qn"] + 1], scalar1=128 ** -0.5, scalar2=None, op0=ALU.mult),
          reads=["vecs"], writes=["qk_g"])
    P.add("dve", lambda e: e.tensor_copy(out=qk_g[:, 1:2], in_=vecs[:, VC["kn"]:VC["kn"] + 1]), reads=["vecs"], writes=["qk_g"])
    base_mark = arena.mark()

    cast_jobs = []

    def cast(dst, src, tok):
        cast_jobs.append((dst, src, tok))

    for og in range(12):
        cast(wb_in[og], w_in[:, og * 512:(og + 1) * 512], f"wb_in{og}")
    for og in range(4):
        cast(wb_out[og], w_out[:, og * 512:(og + 1) * 512], f"wb_out{og}")
    for j in range(22):
        cast(wb_gu[j, :, 0:256], w_gu[:, j * 256:(j + 1) * 256], f"wb_gua{j}")
        cast(wb_gu[j, :, 256:512], w_gu[:, FFN + j * 256:FFN + (j + 1) * 256], f"wb_gug{j}")
    for og in range(8):
        cast(wb_down[og], w_down[:, og * 256:(og + 1) * 256], f"wb_down{og}")
    for og in range(4):
        cast(wb_pg[og], w_pg[:, og * 512:(og + 1) * 512], f"wb_pg{og}")
    cast(wb_pp, w_pp, "wb_pp")

    def emit_casts(n):
        for _ in range(n):
            if not cast_jobs:
                return
            dst, src, tok = cast_jobs.pop(0)
            P.add("pool", lambda e, dst=dst, src=src: e.dma_start(out=dst, in_=src), writes=[tok], dma=True, grp=1)

    emit_casts(12)

    xn = arena.alloc(16 * S, BF16).rearrange("p (k t) -> p k t", k=16)
    TS = 128
    x_sb = [arena.alloc(16 * TS, F32).rearrange("p (k t) -> p k t", k=16) for _ in range(2)]
    sq_sb = [arena.alloc(16 * TS, BF16).rearrange("p (k t) -> p k t", k=16) for _ in range(2)]
    rs_sb = [arena.alloc(TS, F32) for _ in range(2)]
    slab = [arena.alloc(16 * 512, BF16).rearrange("p (k n) -> p k n", k=16) for _ in range(2)]
    stg = [arena.alloc(4 * 512, BF16).rearrange("p (m n) -> p m n", m=4) for _ in range(2)]
    sqq = [arena.alloc(512, BF16) for _ in range(2)]
    rs2 = [arena.alloc(512, F32) for _ in range(2)]
    xTv = xT.rearrange("(k p) t -> p k t", p=128)
    g1 = VC["norm1"]

    def norm_a(tt):
        b = tt % 2
        tsl = slice(tt * TS, (tt + 1) * TS)
        P.add("sp", lambda e: e.dma_start(out=x_sb[b], in_=xN[tt].rearrange("p (k t) -> p k t", k=16)), writes=[f"x_sb{b}"], dma=True)
        P.add("act", lambda e: e.activation(out=sq_sb[b], in_=x_sb[b], func=AF.Square), reads=[f"x_sb{b}"], writes=[f"sq{b}"])
        pb = 6 + b
        for kc in range(16):
            P.add("pe", lambda e, kc=kc: e.matmul(ps[pb][:, 0:TS], lhsT=ones_bf, rhs=sq_sb[b][:, kc, :], start=(kc == 0), stop=(kc == 15)),
                  reads=[f"sq{b}", "ones"], writes=[PS[pb]])

    def norm_b(tt):
        b = tt % 2
        t5 = tt // 4
        tsl = slice(tt * TS, (tt + 1) * TS)
        pb = 6 + b
        P.add("act", lambda e: e.activation(out=rs_sb[b], in_=ps[pb][:, 0:TS], func=AF.Ln, scale=1.0 / D, bias=EPS), writes=[PS[pb], f"rs{b}"])
        P.add("act", lambda e: e.activation(out=rs_sb[b], in_=rs_sb[b], func=AF.Exp, scale=-0.5), writes=[f"rs{b}"])
        for kc in range(16):
            P.add("dve", lambda e, kc=kc: e.scalar_tensor_tensor(out=xn[:, kc, tsl], in0=x_sb[b][:, kc, :], scalar=vecs[:, g1 + kc:g1 + kc + 1],
                                                                 in1=rs_sb[b], op0=ALU.mult, op1=ALU.mult),
                  reads=[f"x_sb{b}", f"rs{b}", "vecs"], writes=[f"xn{t5}_{kc}"])

    qkv = qkT_s.rearrange("(c p) t -> p c t", p=128)
    hyv = hyT_s.rearrange("(c p) t -> p c t", p=128)
    cnt = {"bank": 0, "n": 0, "stg": 0}
    pending = []

    def flush_pending():
        while pending:
            pending.pop(0)()

    def load_slab(og):
        sb = og % 2
        P.add("sp", lambda e: e.dma_start(out=slab[sb], in_=wb_in[og].rearrange("(k p) n -> p k n", p=128)),
              reads=[f"wb_in{og}"], writes=[f"slab{sb}"], dma=True)

    def gemm_tile(og, t):
        sb = og % 2
        tsl = slice(t * 512, (t + 1) * 512)
        si = cnt["stg"] % 2
        cnt["stg"] += 1
        for m in range(4):
            pb = cnt["bank"] % 4
            cnt["bank"] += 1
            for kc in range(16):
                P.add("pe", lambda e, kc=kc, pb=pb, m=m: e.matmul(ps[pb], lhsT=slab[sb][:, kc, m * 128:(m + 1) * 128], rhs=xn[:, kc, tsl],
                                                                 start=(kc == 0), stop=(kc == 15)),
                      reads=[f"slab{sb}", f"xn{t}_{kc}"], writes=[PS[pb]])
            flush_pending()
            if og < 4:
                n = cnt["n"] % 2
                cnt["n"] += 1
                gcol = 0 if og < 2 else 1
                P.add("act", lambda e, pb=pb, n=n: e.activation(out=sqq[n], in_=ps[pb], func=AF.Square), writes=[PS[pb], f"sqq{n}"])

                def post(pb=pb, n=n, si=si, m=m, gcol=gcol):
                    pn = 4 + n
                    P.add("pe", lambda e: e.matmul(ps[pn], lhsT=ones_bf, rhs=sqq[n], start=True, stop=True), reads=[f"sqq{n}", "ones"], writes=[PS[pn]])
                    P.add("act", lambda e: e.activation(out=rs2[n], in_=ps[pn], func=AF.Ln, scale=1.0 / 128, bias=EPS), writes=[PS[pn], f"rs2{n}"])
                    P.add("act", lambda e: e.activation(out=rs2[n], in_=rs2[n], func=AF.Exp, scale=-0.5), writes=[f"rs2{n}"])
                    P.add("dve", lambda e: e.scalar_tensor_tensor(out=stg[si][:, m, :], in0=ps[pb], scalar=qk_g[:, gcol:gcol + 1], in1=rs2[n],
                                                                  op0=ALU.mult, op1=ALU.mult),
                          reads=[f"rs2{n}", "qk_g"], writes=[PS[pb], f"stg{si}"])
                pending.append(post)
            else:
                if m % 2 == 0:
                    P.add("act", lambda e, pb=pb, m=m: e.activation(out=stg[si][:, m, :], in_=ps[pb], func=AF.Copy), writes=[PS[pb], f"stg{si}"])
                else:
                    P.add("dve", lambda e, pb=pb, m=m: e.tensor_copy(out=stg[si][:, m, :], in_=ps[pb]), writes=[PS[pb], f"stg{si}"])
        flush_pending()
        if og < 4:
            dst = qkv[:, og * 4:(og + 1) * 4, tsl]
            tok = [f"qk_s{og * 4 + m}_{t}" for m in range(4)]
        else:
            dst = hyv[:, (og - 6) * 4:(og - 5) * 4, tsl]
            tok = [f"hy_s{(og - 6) * 4 + m}_{t}" for m in range(4)]
        P.add("pool", lambda e: e.dma_start(out=dst, in_=stg[si]), reads=[f"stg{si}"], writes=tok, dma=True)

    load_slab(0)
    load_slab(1)
    norm_a(0)
    for t in range(8):
        for tt in range(4 * t, 4 * t + 4):
            if tt + 1 < 32:
                norm_a(tt + 1)
            norm_b(tt)
        gemm_tile(0, t)
    emit_casts(6)
    for og in range(1, 12):
        sb = og % 2
        if og >= 2:
            load_slab(og)
        if og in (4, 5):
            for ts_ in range(32):
                pb = cnt["bank"] % 4
                cnt["bank"] += 1
                xr = [f"xn{ts_ // 4}_{kc}" for kc in range(16)]
                for kc in range(16):
                    P.add("pe", lambda e, kc=kc, pb=pb, sb=sb, ts_=ts_: e.matmul(ps[pb], lhsT=xn[:, kc, ts_ * 128:(ts_ + 1) * 128], rhs=slab[sb][:, kc, :],
                                                                                 start=(kc == 0), stop=(kc == 15)),
                          reads=[f"slab{sb}", xr[kc]], writes=[PS[pb]])
                flush_pending()
                si = cnt["stg"] % 2
                cnt["stg"] += 1
                eng = "act" if ts_ % 2 == 0 else "dve"
                if eng == "act":
                    P.add("act", lambda e, pb=pb, si=si: e.activation(out=stg[si][:, 0, :], in_=ps[pb], func=AF.Copy), writes=[PS[pb], f"stg{si}"])
                else:
                    P.add("dve", lambda e, pb=pb, si=si: e.tensor_copy(out=stg[si][:, 0, :], in_=ps[pb]), writes=[PS[pb], f"stg{si}"])
                P.add("pool", lambda e, si=si, ts_=ts_, og=og: e.dma_start(out=v_s[ts_ * 128:(ts_ + 1) * 128, (og - 4) * 512:(og - 3) * 512], in_=stg[si][:, 0, :]),
                      reads=[f"stg{si}"], writes=[f"v_s{ts_}_{og}"], dma=True)
            continue
        for t in range(8):
            gemm_tile(og, t)
        emit_casts(6)
    emit_casts(1000)
    if stop_after <= 1:
        return nc, P
    P.barrier()
    arena.reset(base_mark)
    if stop_after >= 2 and not y_from_input:
        LT = 384
        MAGIC = 12582912.0
        PI_LO = 3.1415925
        TWO_PI = 2.0 * math.pi
        ident = arena.alloc(128, BF16)
        P.add("pool", lambda e: e.memset(ident, 0.0), writes=["ident"])
        P.add("pool", lambda e: e.affine_select(out=ident, in_=ident, pattern=[[-1, 128]], compare_op=ALU.not_equal, fill=1.0, base=0, channel_multiplier=1), writes=["ident"])
        if il_mode == "il":
            arA = arena.sub(86 * 1024)
            arH = arena.sub(arena.remaining())
            AT_S, AT_O, AT_L, PD = [0, 1], [2, 2], [3, 3], 1
            HY_BANKS = [4, 5, 6, 7]
        else:
            mk0 = arena.mark()
            arA = arena.sub(arena.remaining())
            arena.reset(mk0)
            arH = arena.sub(arena.remaining())
            AT_S, AT_O, AT_L, PD = [0, 1, 2, 3], [4, 5], [6, 7], 2
            HY_BANKS = list(range(8))

        def attn_thread():
            ar = arA
            masks = [[ar.alloc(256, BF16) for g in range(3)] for h in range(NH)]
            accO = ar.alloc(S, F32)
            accL = ar.alloc(S, F32)
            mk = ar.mark()
            relb_sb = ar.alloc(8, F32, parts=32)
            oh_sb = ar.alloc(3 * LT, F32, parts=32)
            valid_sb = ar.alloc(3 * LT, F32, parts=8)
            cneg_sb = ar.alloc(3 * LT, F32, parts=8)
            sel_sb = ar.alloc(8 * 128, F32, parts=8)
            e_sb = ar.alloc(3 * LT, F32, parts=8)
            mrow = ar.alloc(24 * LT, BF16).rearrange("p (n l) -> p n l", l=LT)
            P.add("sp", lambda e: e.dma_start(out=relb_sb, in_=relb_d), writes=["relb"], dma=True)
            P.add("sp", lambda e: e.dma_start(out=oh_sb, in_=oh_d), writes=["oh"], dma=True)
            P.add("sp", lambda e: e.dma_start(out=valid_sb, in_=valid_d), writes=["valid"], dma=True)
            P.add("sp", lambda e: e.dma_start(out=cneg_sb, in_=cneg_d), writes=["cneg"], dma=True)
            P.add("sp", lambda e: e.dma_start(out=sel_sb, in_=sel_d), writes=["sel"], dma=True)
            for g in range(3):
                pb = g % 2
                P.add("pe", lambda e, g=g, pb=pb: e.matmul(ps[pb][0:8, 0:LT], lhsT=relb_sb, rhs=oh_sb[:, g * LT:(g + 1) * LT], start=True, stop=True),
                      reads=["relb", "oh"], writes=[PS[pb]])
                P.add("dve", lambda e, g=g, pb=pb: e.tensor_tensor(out=e_sb[:, g * LT:(g + 1) * LT], in0=ps[pb][0:8, 0:LT], in1=valid_sb[:, g * LT:(g + 1) * LT], op=ALU.mult),
                      reads=["valid"], writes=[PS[pb], "e_sb"])
            P.add("dve", lambda e: e.tensor_tensor(out=e_sb, in0=e_sb, in1=cneg_sb, op=ALU.add), reads=["cneg"], writes=["e_sb"])
            yield
            for h in range(NH):
                for g in range(3):
                    pb = (h * 3 + g) % 2
                    P.add("pe", lambda e, h=h, g=g, pb=pb: e.matmul(ps[pb][:, 0:LT], lhsT=sel_sb[:, h * 128:(h + 1) * 128], rhs=e_sb[:, g * LT:(g + 1) * LT], start=True, stop=True),
                          reads=["sel", "e_sb"], writes=[PS[pb]])
                    P.add("dve", lambda e, pb=pb, h=h, g=g: e.tensor_copy(out=mrow[:, h * 3 + g, :], in_=ps[pb][:, 0:LT]), writes=[PS[pb], "mrow"])
                yield
            mrow_dma = P.add("sp", lambda e: e.dma_start(out=mvec_s.rearrange("n (p l) -> p n l", l=LT), in_=mrow), reads=["mrow"], writes=["mvec"], dma=True)
            for h in range(NH):
                for g in range(3):
                    P.add("sp", lambda e, h=h, g=g: e.dma_start(out=masks[h][g], in_=bass.AP(mvec_s.tensor, (h * 3 + g) * 128 * LT, [[LT - 1, 128], [1, 256]])),
                          reads=["mvec"], writes=[f"mask{h}_{g}"], dma=True)
            yield
            ar.reset(mk)
            qk_sb = [ar.alloc(2 * S, BF16).rearrange("p (a t) -> p a t", a=2) for _ in range(1)]
            v_sb = [ar.alloc(32 * 128, BF16).rearrange("p (s c) -> p s c", s=32) for _ in range(2)]
            pt_sb = [ar.alloc(256, BF16) for _ in range(4)]
            aysq = [ar.alloc(512, BF16) for _ in range(2)]
            ayrs = [ar.alloc(512, F32) for _ in range(2)]
            aystg = [ar.alloc(512, BF16) for _ in range(2)]
            a2 = {"s": 0, "pt": 0, "v": 0, "grp": 0, "y": 0}
            last_pv = [None, None]
            qkTv = qkT_s.rearrange("(a h p) t -> h p a t", a=2, p=128)
            for h in range(NH):
                qb = 0
                P.add("sp", lambda e, h=h, qb=qb: e.dma_start(out=qk_sb[qb], in_=qkTv[h]),
                      reads=[f"qk_s{c}_{t}" for c in (h, 8 + h) for t in range(8)], writes=[f"qk_sb{qb}"], dma=True, deps=[mrow_dma])
                for g, d in enumerate((1, 4, 16)):
                    Ls = S // d
                    G = min(512, Ls)
                    nkb = Ls // 128
                    vb = a2["v"] % 2
                    a2["v"] += 1
                    for r in range(d):
                        vsrc = bass.AP(v_s.tensor, h * 128 + r * 1024, [[d * 1024, 128], [128 * d * 1024, nkb], [1, 128]])
                        P.add("sp", lambda e, vb=vb, vsrc=vsrc, r=r, d=d: e.dma_start(out=v_sb[vb][:, r::d, :], in_=vsrc),
                              reads=[f"v_s{ts_}_{4 + h // 4}" for ts_ in range(32)], writes=[f"v_sb{vb}_{r}"], dma=True, deps=[last_pv[vb], mrow_dma])
                    for r in range(d):
                        for grp in range(Ls // G):
                            g0 = grp * G
                            po = AT_O[a2["grp"] % 2]
                            pl = AT_L[a2["grp"] % 2]
                            a2["grp"] += 1
                            kbs = [kb for kb in range(g0 // 128 - 1, g0 // 128 + G // 128 + 1) if 0 <= kb < nkb]
                            pend = []
                            first = [True]

                            def pv(kb, pti, qlo, nq, po=po, pl=pl, vb=vb, r=r, d=d, g0=g0, first=first):
                                st = first[0]
                                first[0] = False
                                osl = slice(qlo - g0, qlo - g0 + nq)
                                last_pv[vb] = P.add("pe", lambda e: e.matmul(ps[po][:, osl], lhsT=v_sb[vb][:, kb * d + r, :], rhs=pt_sb[pti][:, 0:nq], start=st, stop=False, skip_group_check=True),
                                                    reads=[f"v_sb{vb}_{r}", f"pt{pti}"], writes=[PS[po]])
                                P.add("pe", lambda e: e.matmul(ps[pl][:, osl], lhsT=ones_bf, rhs=pt_sb[pti][:, 0:nq], start=st, stop=False, skip_group_check=True),
                                      reads=["ones", f"pt{pti}"], writes=[PS[pl]])

                            for kb in kbs:
                                qlo = max(g0, 128 * kb - 64)
                                qhi = min(g0 + G, 128 * kb + 192)
                                nq = qhi - qlo
                                wlo = qlo - (128 * kb - 64)
                                pb = AT_S[a2["s"] % len(AT_S)]
                                a2["s"] += 1
                                pti = a2["pt"] % 4
                                a2["pt"] += 1
                                ksl = slice(r + d * 128 * kb, r + d * (128 * kb + 127) + 1, d)
                                qsl = slice(r + d * qlo, r + d * (qhi - 1) + 1, d)
                                P.add("pe", lambda e, pb=pb, qb=qb, ksl=ksl, qsl=qsl, nq=nq: e.matmul(ps[pb][:, 0:nq], lhsT=qk_sb[qb][:, 1, ksl], rhs=qk_sb[qb][:, 0, qsl], start=True, stop=False),
                                      reads=[f"qk_sb{qb}"], writes=[PS[pb]])
                                P.add("pe", lambda e, pb=pb, nq=nq, wlo=wlo, h=h, g=g: e.matmul(ps[pb][:, 0:nq], lhsT=ident, rhs=masks[h][g][:, wlo:wlo + nq], start=False, stop=True),
                                      reads=[f"mask{h}_{g}", "ident"], writes=[PS[pb]])
                                P.add("act", lambda e, pb=pb, pti=pti, nq=nq: e.activation(out=pt_sb[pti][:, 0:nq], in_=ps[pb][:, 0:nq], func=AF.Exp), writes=[PS[pb], f"pt{pti}"],
                                      deps=[mrow_dma])
                                pend.append((kb, pti, qlo, nq))
                                if len(pend) > PD:
                                    pv(*pend.pop(0))
                            while pend:
                                pv(*pend.pop(0))
                            asl = slice(r + d * g0, r + d * (g0 + G - 1) + 1, d)
                            if g == 0:
                                P.add("act", lambda e, po=po, asl=asl, G=G: e.activation(out=accO[:, asl], in_=ps[po][:, 0:G], func=AF.Copy), writes=[PS[po], "accO"])
                                P.add("dve", lambda e, pl=pl, asl=asl, G=G: e.tensor_copy(out=accL[:, asl], in_=ps[pl][:, 0:G]), writes=[PS[pl], "accL"])
                            else:
                                P.add("dve", lambda e, po=po, asl=asl, G=G: e.tensor_tensor(out=accO[:, asl], in0=ps[po][:, 0:G], in1=accO[:, asl], op=ALU.add), writes=[PS[po], "accO"])
                                P.add("dve", lambda e, pl=pl, asl=asl, G=G: e.tensor_tensor(out=accL[:, asl], in0=ps[pl][:, 0:G], in1=accL[:, asl], op=ALU.add), writes=[PS[pl], "accL"])
                            yield
                ga = VC["attn_on"] + h

                def af0(c):
                    yb = c % 2
                    csl = slice(c * 512, (c + 1) * 512)
                    P.add("dve", lambda e: e.reciprocal(out=accL[:, csl], in_=accL[:, csl]), writes=["accL"])
                    P.add("dve", lambda e: e.tensor_tensor(out=accO[:, csl], in0=accO[:, csl], in1=accL[:, csl], op=ALU.mult), reads=["accL"], writes=["accO"])
                    P.add("act", lambda e: e.activation(out=aysq[yb], in_=accO[:, csl], func=AF.Square), reads=["accO"], writes=[f"aysq{yb}"], deps=[mrow_dma])

                def af1(c):
                    yb = c % 2
                    P.add("pe", lambda e: e.matmul(ps[yb], lhsT=ones_bf, rhs=aysq[yb], start=True, stop=True), reads=[f"aysq{yb}", "ones"], writes=[PS[yb]])

                def af2(c):
                    yb = c % 2
                    P.add("act", lambda e: e.activation(out=ayrs[yb], in_=ps[yb], func=AF.Ln, scale=1.0 / 128, bias=EPS), writes=[PS[yb], f"ayrs{yb}"])
                    P.add("act", lambda e: e.activation(out=ayrs[yb], in_=ayrs[yb], func=AF.Exp, scale=-0.5), writes=[f"ayrs{yb}"])

                def af3(c, h=h, ga=ga):
                    yb = c % 2
                    csl = slice(c * 512, (c + 1) * 512)
                    P.add("dve", lambda e: e.scalar_tensor_tensor(out=aystg[yb], in0=accO[:, csl], scalar=vecs[:, ga:ga + 1], in1=ayrs[yb], op0=ALU.mult, op1=ALU.mult),
                          reads=["accO", f"ayrs{yb}", "vecs"], writes=[f"aystg{yb}"])
                    P.add("pool", lambda e: e.dma_start(out=yT_s[h * 128:(h + 1) * 128, csl], in_=aystg[yb]), reads=[f"aystg{yb}"], writes=[f"y_s{h}_{c}"], dma=True)

                yield from pipeline(8, [af0, af1, af2, af3])

        def pipeline(n, stages, lag=1):
            ns = len(stages)
            for step in range(n + (ns - 1) * lag):
                for k, f in enumerate(stages):
                    i = step - k * lag
                    if 0 <= i < n:
                        f(i)
                yield

        def hyena_thread():
            ar = arH
            HBK = HY_BANKS
            NBK = len(HBK)
            fa_f = ar.alloc(128, F32, parts=32)
            fa_bf = ar.alloc(128, BF16, parts=32)
            fai_f = ar.alloc(32, F32)
            fai_bf = ar.alloc(32, BF16)
            P.add("sp", lambda e: e.dma_start(out=fa_f, in_=cfa_d), writes=["fa_f"], dma=True)
            P.add("sp", lambda e: e.dma_start(out=fai_f, in_=cfai_d), writes=["fai_f"], dma=True)
            P.add("dve", lambda e: e.tensor_copy(out=fa_bf, in_=fa_f), reads=["fa_f"], writes=["fa_bf"])
            P.add("dve", lambda e: e.tensor_copy(out=fai_bf, in_=fai_f), reads=["fai_f"], writes=["fai_bf"])
            for i in range(8):
                P.add("pool", lambda e, i=i: e.dma_start(out=cB_s[i * 8:(i + 1) * 8], in_=cb_d[i * 8:(i + 1) * 8]), writes=[f"cB{i}"], dma=True)
            hy_base = ar.mark()
            TMH = 2 if ar.remaining() < 150 * 1024 else 1
            TMC = 1024 // TMH

            def tile_to_tm_pe(tile_ap, tiletok, pb):
                pv_ = ps[pb].bitcast(BF16)
                for q in range(4):
                    P.add("pe", lambda e, pv_=pv_, q=q: e.transpose(pv_[:, q * 128:(q + 1) * 128], tile_ap[:, q * 128:(q + 1) * 128], ident),
                          reads=[tiletok, "ident"], writes=[PS[pb]])

            def tile_to_tm_copy(pb, pc, cl, tm, dtok, par):
                pv_ = ps[pb].bitcast(BF16)
                if par % 2 == 0:
                    P.add("act", lambda e: e.activation(out=tm[:, pc * 4:(pc + 1) * 4, cl * 128:(cl + 1) * 128], in_=pv_[:, 0:512].rearrange("p (q c) -> p q c", q=4), func=AF.Copy),
                          writes=[PS[pb], f"{dtok}tm"])
                else:
                    P.add("dve", lambda e: e.tensor_copy(out=tm[:, pc * 4:(pc + 1) * 4, cl * 128:(cl + 1) * 128], in_=pv_[:, 0:512].rearrange("p (q c) -> p q c", q=4)),
                          writes=[PS[pb], f"{dtok}tm"])

            def store_tm(tm, dst_dram, dtok, chalf):
                for hb_ in range(4):
                    P.add("pool", lambda e, hb_=hb_: e.dma_start(out=dst_dram.rearrange("(b p) c -> p b c", p=128)[:, hb_ * 8:(hb_ + 1) * 8, chalf * TMC:(chalf + 1) * TMC], in_=tm[:, hb_ * 8:(hb_ + 1) * 8, :]),
                          reads=[f"{dtok}tm"], writes=[f"{dtok}{chalf}_{hb_}"], dma=True)

            w1_sb = ar.alloc(64, F32, parts=33)
            wi_sb = ar.alloc(128, F32, parts=64)
            wo_sb = ar.alloc(1024, F32, parts=64)
            fb_sb = ar.alloc(4, F32, parts=64)
            zfc = [ar.alloc(512, F32, parts=33) for _ in range(3)]
            hdn = [ar.alloc(S, F32, parts=64) for _ in range(2)]
            kk = [ar.alloc(512, F32, parts=64) for _ in range(2)]
            offc = [ar.alloc(512, F32) for _ in range(8)]
            ndec = ar.alloc(8, F32)
            NW = 4
            win = [ar.alloc(512, F32) for _ in range(NW)]
            ftile = [ar.alloc(512, BF16) for _ in range(NW)]
            tokmaj = ar.alloc(32 * TMC, BF16).rearrange("p (b c) -> p b c", b=32)
            wo_bf = ar.alloc(1024, BF16, parts=64)
            hfin_bf = ar.alloc(S, BF16, parts=64)
            P.add("sp", lambda e: e.dma_start(out=w1_sb, in_=hw1_d), writes=["w1"], dma=True)
            P.add("sp", lambda e: e.dma_start(out=wi_sb.rearrange("p (j o) -> p j o", j=2), in_=hwi_d.rearrange("j i o -> i j o")), writes=["wi"], dma=True)
            P.add("sp", lambda e: e.dma_start(out=wo_sb, in_=hwo_d), writes=["wo"], dma=True)
            for pc in range(8):
                P.add("sp", lambda e, pc=pc: e.dma_start(out=offc[pc], in_=bass.AP(coffs_d.tensor, pc * 512, [[0, 128], [1, 512]])), writes=[f"offc{pc}"], dma=True)
            fq = VC["freq"]
            for j, nm in enumerate(("b1", "bi0", "bi1")):
                P.add("dve", lambda e, j=j, nm=nm: e.tensor_tensor(out=fb_sb[:, j:j + 1], in0=vecs[0:64, VC[nm]:VC[nm] + 1], in1=vecs[0:64, fq:fq + 1], op=ALU.mult),
                      reads=["vecs"], writes=["fb"])
            P.add("act", lambda e: e.activation(out=ndec, in_=vecs[:, VC["decay"]:VC["decay"] + 8], func=AF.Abs), reads=["vecs"], writes=["ndec"])
            P.add("dve", lambda e: e.tensor_scalar(out=ndec, in0=ndec, scalar1=-1.0, scalar2=None, op0=ALU.mult), writes=["ndec"])
            for layer in range(3):
                dst = hdn[layer % 2]
                src = hdn[(layer + 1) % 2]
                dtok = [f"hdn{layer % 2}_{pc}" for pc in range(8)]
                stok = [f"hdn{(layer + 1) % 2}_{pc}" for pc in range(8)]

                def m1(pc, layer=layer, src=src, stok=stok):
                    psl = slice(pc * 512, (pc + 1) * 512)
                    pb = HBK[pc % NBK]
                    if layer == 0:
                        zb_ = pc % 3
                        P.add("sp", lambda e: e.dma_start(out=zfc[zb_], in_=czf_d[:, psl]), writes=[f"zfc{zb_}"], dma=True)
                        P.add("pe", lambda e: e.matmul(ps[pb][0:64, :], lhsT=w1_sb, rhs=zfc[zb_], start=True, stop=True), reads=["w1", f"zfc{zb_}"], writes=[PS[pb]])
                    else:
                        P.add("pe", lambda e: e.matmul(ps[pb][0:64, :], lhsT=wi_sb[:, (layer - 1) * 64:layer * 64], rhs=src[:, psl], start=True, stop=True),
                              reads=["wi", stok[pc]], writes=[PS[pb]])

                def m2(pc, layer=layer, dst=dst, dtok=dtok):
                    psl = slice(pc * 512, (pc + 1) * 512)
                    pb = HBK[pc % NBK]
                    P.add("act", lambda e: e.activation(out=dst[:, psl], in_=ps[pb][0:64, :], func=AF.Identity, scale=vecs[0:64, fq:fq + 1], bias=fb_sb[:, layer:layer + 1]),
                          reads=["vecs", "fb"], writes=[PS[pb], dtok[pc]])

                def m3(pc, dst=dst, dtok=dtok):
                    psl = slice(pc * 512, (pc + 1) * 512)
                    k_ = kk[pc % 2]
                    kt = f"kk{pc % 2}"
                    P.add("dve", lambda e: e.tensor_scalar(out=k_, in0=dst[:, psl], scalar1=1.0 / TWO_PI, scalar2=MAGIC, op0=ALU.mult, op1=ALU.add), reads=[dtok[pc]], writes=[kt])
                    P.add("dve", lambda e: e.tensor_scalar(out=k_, in0=k_, scalar1=MAGIC, scalar2=None, op0=ALU.subtract), writes=[kt])
                    P.add("dve", lambda e: e.scalar_tensor_tensor(out=dst[:, psl], in0=k_, scalar=-TWO_PI, in1=dst[:, psl], op0=ALU.mult, op1=ALU.add), reads=[kt], writes=[dtok[pc]])
                    P.add("dve", lambda e: e.tensor_scalar(out=dst[:, psl], in0=dst[:, psl], scalar1=-PI_LO, scalar2=PI_LO, op0=ALU.max, op1=ALU.min), writes=[dtok[pc]])

                def m4(pc, dst=dst, dtok=dtok):
                    psl = slice(pc * 512, (pc + 1) * 512)
                    P.add("act", lambda e: e.activation(out=dst[:, psl], in_=dst[:, psl], func=AF.Sin), writes=[dtok[pc]])

                yield from pipeline(8, [m1, m2, m3, m4])
            hfin = hfin_bf
            P.add("pool", lambda e: e.tensor_copy(out=wo_bf, in_=wo_sb), reads=["wo"], writes=["wo"])
            for pc in range(8):
                P.add("pool" if pc % 2 else "dve", lambda e, pc=pc: e.tensor_copy(out=hfin_bf[:, pc * 512:(pc + 1) * 512], in_=hdn[0][:, pc * 512:(pc + 1) * 512]),
                      reads=[f"hdn0_{pc}"], writes=[f"hfb_{pc}"])
            HF_T = [f"hfb_{pc}" for pc in range(8)]

            for chalf in range(TMH):
                ncl = 8 // TMH

                def units(u, chalf=chalf, ncl=ncl):
                    cl, pc = divmod(u, 8)
                    return chalf * ncl + cl, cl, pc

                def f1(u):
                    c, cl, pc = units(u)
                    pb = HBK[u % (NBK // 2)]
                    psl = slice(pc * 512, (pc + 1) * 512)
                    P.add("pe", lambda e: e.matmul(ps[pb], lhsT=wo_bf[:, c * 128:(c + 1) * 128], rhs=hfin[:, psl], start=True, stop=True),
                          reads=["wo", HF_T[pc]], writes=[PS[pb]])
                    fi = u % NW
                    P.add("act", lambda e: e.activation(out=win[fi], in_=offc[pc], func=AF.Exp, scale=ndec[:, c:c + 1]), reads=[f"offc{pc}", "ndec"], writes=[f"win{fi}"])

                def f2(u):
                    c, cl, pc = units(u)
                    pb = HBK[u % (NBK // 2)]
                    fi = u % NW
                    P.add("dve", lambda e: e.tensor_tensor(out=ftile[fi], in0=ps[pb], in1=win[fi], op=ALU.mult), reads=[f"win{fi}"], writes=[PS[pb], f"ftile{fi}"])

                def f3(u):
                    fi = u % NW
                    tile_to_tm_pe(ftile[fi], f"ftile{fi}", HBK[NBK // 2 + u % (NBK // 2)])

                def f4(u):
                    c, cl, pc = units(u)
                    tile_to_tm_copy(HBK[NBK // 2 + u % (NBK // 2)], pc, cl, tokmaj, "filt_s", u)

                yield from pipeline(8 * ncl, [f1, f2, f3, f4])
                store_tm(tokmaj, filt_s, "filt_s", chalf)
                yield
            FILT_ALL = [f"filt_s{ch}_{hb_}" for ch in range(TMH) for hb_ in range(4)]

            ar.reset(hy_base)
            P.barrier()
            diag = ar.alloc(72 * 128, BF16).rearrange("p (j c) -> p j c", j=72)
            NU = 6
            u_sb = [ar.alloc(S + 2, BF16) for _ in range(NU)]
            NC_ = 4
            ctmp = [ar.alloc(512, F32) for _ in range(NC_)]
            zt_t = [ar.alloc(512, BF16) for _ in range(NC_)]
            x0st = [ar.alloc(512, BF16) for _ in range(NC_)]
            tokmaj2 = ar.alloc(32 * TMC, BF16).rearrange("p (b c) -> p b c", b=32)
            for ch in range(24):
                for j in range(3):
                    col = VC[f"cw{j}"] + ch
                    P.add("dve", lambda e, ch=ch, j=j, col=col: e.tensor_scalar(out=diag[:, ch * 3 + j, :], in0=ident, scalar1=vecs[:, col:col + 1], scalar2=None, op0=ALU.mult),
                          reads=["ident", "vecs"], writes=["diag"])
            for i in range(NU):
                P.add("pool", lambda e, i=i: e.memset(u_sb[i][:, 0:1], 0.0), writes=[f"u{i}"])
                P.add("pool", lambda e, i=i: e.memset(u_sb[i][:, S + 1:S + 2], 0.0), writes=[f"u{i}"])
            yield
            hyTv = hyT_s.rearrange("(c p) t -> c p t", p=128)
            zTv = zT_s.rearrange("(c p) t -> c p t", p=128)
            x0Tv = x0T_s.rearrange("(c p) t -> c p t", p=128)
            bset = NBK // 4

            def bank(role, u):
                return HBK[role * bset + u % bset]

            def ubuf(c, which):
                return (c % 2) * 3 + which

            def b0(u):
                c, pc = divmod(u, 8)
                if pc == 0:
                    for which, ch in enumerate((8 + c, 16 + c, c)):
                        ub = ubuf(c, which)
                        P.add("sp", lambda e, ub=ub, ch=ch: e.dma_start(out=u_sb[ub][:, 1:S + 1], in_=hyTv[ch]), reads=[f"hy_s{ch}_{t}" for t in range(8)], writes=[f"u{ub}"], dma=True)

            def conv_mm(ch, ub, pc, pb):
                for j in range(3):
                    P.add("pe", lambda e, j=j: e.matmul(ps[pb], lhsT=diag[:, ch * 3 + j, :], rhs=u_sb[ub][:, pc * 512 + j:pc * 512 + j + 512], start=(j == 0), stop=(j == 2)),
                          reads=["diag", f"u{ub}"], writes=[PS[pb]])

            def b1(u):
                c, pc = divmod(u, 8)
                conv_mm(8 + c, ubuf(c, 0), pc, bank(0, u))
                conv_mm(16 + c, ubuf(c, 1), pc, bank(1, u))
                conv_mm(c, ubuf(c, 2), pc, bank(2, u))

            def b2(u):
                c, pc = divmod(u, 8)
                ti = u % NC_
                b1c = VC["cb"] + 8 + c
                b0c = VC["cb"] + c
                pa, px = bank(0, u), bank(2, u)
                psl = slice(pc * 512, (pc + 1) * 512)
                P.add("act", lambda e: e.activation(out=ctmp[ti], in_=ps[pa], func=AF.Identity, bias=vecs[:, b1c:b1c + 1]), reads=["vecs"], writes=[PS[pa], f"ctmp{ti}"])
                P.add("act", lambda e: e.activation(out=x0st[ti], in_=ps[px], func=AF.Identity, bias=vecs[:, b0c:b0c + 1]), reads=["vecs"], writes=[PS[px], f"x0st{ti}"])
                P.add("pool", lambda e: e.dma_start(out=x0Tv[c][:, psl], in_=x0st[ti]), reads=[f"x0st{ti}"], writes=[f"x0T_s{c}_{pc}"], dma=True)

            def b3(u):
                c, pc = divmod(u, 8)
                ti = u % NC_
                bvc = VC["cb"] + 16 + c
                pb2 = bank(1, u)
                psl = slice(pc * 512, (pc + 1) * 512)
                P.add("dve", lambda e: e.scalar_tensor_tensor(out=zt_t[ti], in0=ps[pb2], scalar=vecs[:, bvc:bvc + 1], in1=ctmp[ti], op0=ALU.add, op1=ALU.mult),
                      reads=[f"ctmp{ti}", "vecs"], writes=[PS[pb2], f"zt_t{ti}"])
                P.add("pool", lambda e: e.dma_start(out=zTv[c][:, psl], in_=zt_t[ti]), reads=[f"zt_t{ti}"], writes=[f"zT_s{c}_{pc}"], dma=True)

            def b4(u):
                ti = u % NC_
                tile_to_tm_pe(zt_t[ti], f"zt_t{ti}", bank(3, u))

            def b5(u):
                c, pc = divmod(u, 8)
                tile_to_tm_copy(bank(3, u), pc, c % (8 // TMH), tokmaj2, "z_s", u)

            for chalf in range(TMH):
                n_ = 64 // TMH
                off_ = chalf * n_
                yield from pipeline(n_, [lambda i, o=off_: b0(i + o), lambda i, o=off_: b1(i + o), lambda i, o=off_: (b2(i + o), b3(i + o)),
                                         lambda i, o=off_: b4(i + o), lambda i, o=off_: b5(i + o)])
                store_tm(tokmaj2, z_s, "z_s", chalf)
                yield
            Z_ALL = [f"z_s{ch}_{hb_}" for ch in range(TMH) for hb_ in range(4)]

            ar.reset(hy_base)
            P.barrier()
            NB = 2 if il_mode == "il" else 4
            in_sb = [ar.alloc(NB * 1024, BF16, parts=32) for _ in range(2)]
            a_out = [ar.alloc(NB * 1024, BF16) for _ in range(2)]
            cA = [0]

            def step_a(src_dram, srctoks):
                srcv = src_dram.rearrange("(n1 n2) c -> n1 (n2 c)", n1=32)
                for blk in range(128 // NB):
                    bi = cA[0] % 2
                    cA[0] += 1
                    P.add("sp", lambda e, bi=bi, blk=blk: e.dma_start(out=in_sb[bi], in_=srcv[:, blk * NB * 1024:(blk + 1) * NB * 1024]), reads=srctoks, writes=[f"in_sb{bi}"], dma=True)
                    for cc in range(NB * 2):
                        pb = HBK[(blk * NB * 2 + cc) % NBK]
                        csl = slice(cc * 512, (cc + 1) * 512)
                        P.add("pe", lambda e, pb=pb, bi=bi, csl=csl: e.matmul(ps[pb], lhsT=fa_bf, rhs=in_sb[bi][:, csl], start=True, stop=True), reads=["fa_bf", f"in_sb{bi}"], writes=[PS[pb]])
                        if cc % 2 == 0:
                            P.add("act", lambda e, pb=pb, bi=bi, csl=csl: e.activation(out=a_out[bi][:, csl], in_=ps[pb], func=AF.Copy), writes=[PS[pb], f"a_out{bi}"])
                        else:
                            P.add("dve", lambda e, pb=pb, bi=bi, csl=csl: e.tensor_copy(out=a_out[bi][:, csl], in_=ps[pb]), writes=[PS[pb], f"a_out{bi}"])
                    P.add("pool", lambda e, bi=bi, blk=blk: e.dma_start(out=A_s[:, blk * NB:(blk + 1) * NB, :].rearrange("m n c -> m (n c)"), in_=a_out[bi]),
                          reads=[f"a_out{bi}"], writes=[f"A_{blk}"], dma=True)
                    yield

            A_ALL = [f"A_{blk}" for blk in range(128 // NB)]
            yield from step_a(filt_s, FILT_ALL)
            NKB = 4
            ar_sb = [ar.alloc(2 * 1024, BF16).rearrange("p (r c) -> p r c", r=2) for _ in range(NKB)]
            cb_sb = [ar.alloc(768, BF16) for _ in range(NKB)]
            h_t = [ar.alloc(2 * 1024, BF16).rearrange("p (r c) -> p r c", r=2) for _ in range(NKB)]
            NX = 3
            x_t = [ar.alloc(2 * 512, BF16).rearrange("p (r c) -> p r c", r=2) for _ in range(NX)]
            tt_ = [ar.alloc(4 * 512, BF16).rearrange("p (r c) -> p r c", r=4) for _ in range(NX)]
            y_t = [ar.alloc(2 * 512, BF16).rearrange("p (r c) -> p r c", r=2) for _ in range(NX)]
            g_t = [ar.alloc(2 * 1024, BF16).rearrange("p (r c) -> p r c", r=2) for _ in range(NKB)]
            BR, BI, NBI, BRT, BIT, NBIT = [slice(i * 128, (i + 1) * 128) for i in range(6)]
            nxs = max(1, NBK // 4)

            def xbanks(i):
                s_ = i % nxs
                return HBK[2 * s_], HBK[2 * s_ + 1]

            def gbanks(i):
                s_ = i % nxs
                return HBK[NBK // 2 + 2 * s_], HBK[NBK // 2 + 2 * s_ + 1]

            def load_k1(k1, b, with_h):
                P.add("sp", lambda e: e.dma_start(out=ar_sb[b][:, 0, :], in_=A_s[k1]), reads=A_ALL, writes=[f"ar{b}r"], dma=True)
                P.add("sp", lambda e: e.dma_start(out=ar_sb[b][:, 1, :], in_=A_s[64 + k1]), reads=A_ALL, writes=[f"ar{b}i"], dma=True)
                P.add("sp", lambda e: e.dma_start(out=cb_sb[b], in_=cB_s[k1]), reads=[f"cB{k1 // 8}"], writes=[f"cb{b}"], dma=True)
                if with_h:
                    P.add("sp", lambda e: e.dma_start(out=h_t[b], in_=Hf_s[k1]), reads=[f"Hf{k1}"], writes=[f"h_t{b}"], dma=True)

            def fwd_b(i):
                k1, half = divmod(i, 2)
                b = k1 % NKB
                pr, pi_ = xbanks(i)
                hs = slice(half * 512, (half + 1) * 512)
                rd = [f"ar{b}r", f"ar{b}i", f"cb{b}"]
                P.add("pe", lambda e: e.matmul(ps[pr], lhsT=cb_sb[b][:, BR], rhs=ar_sb[b][:, 0, hs], start=True, stop=False), reads=rd, writes=[PS[pr]])
                P.add("pe", lambda e: e.matmul(ps[pr], lhsT=cb_sb[b][:, NBI], rhs=ar_sb[b][:, 1, hs], start=False, stop=True), reads=rd, writes=[PS[pr]])
                P.add("pe", lambda e: e.matmul(ps[pi_], lhsT=cb_sb[b][:, BR], rhs=ar_sb[b][:, 1, hs], start=True, stop=False), reads=rd, writes=[PS[pi_]])
                P.add("pe", lambda e: e.matmul(ps[pi_], lhsT=cb_sb[b][:, BI], rhs=ar_sb[b][:, 0, hs], start=False, stop=True), reads=rd, writes=[PS[pi_]])

            def l0(i):
                k1, half = divmod(i, 2)
                if half == 0:
                    load_k1(k1, k1 % NKB, False)

            def l2(i):
                k1, half = divmod(i, 2)
                b = k1 % NKB
                pr, pi_ = xbanks(i)
                hs = slice(half * 512, (half + 1) * 512)
                P.add("act", lambda e: e.activation(out=h_t[b][:, 0, hs], in_=ps[pr], func=AF.Copy), writes=[PS[pr], f"h_t{b}"])
                P.add("dve", lambda e: e.tensor_copy(out=h_t[b][:, 1, hs], in_=ps[pi_]), writes=[PS[pi_], f"h_t{b}"])
                if half == 1:
                    P.add("pool", lambda e: e.dma_start(out=Hf_s[k1], in_=h_t[b]), reads=[f"h_t{b}"], writes=[f"Hf{k1}"], dma=True)

            yield from pipeline(128, [l0, fwd_b, l2])
            yield from step_a(z_s, Z_ALL)

            def z0(i):
                k1, half = divmod(i, 2)
                if half == 0:
                    load_k1(k1, k1 % NKB, True)

            def z2(i):
                pr, pi_ = xbanks(i)
                s_ = i % NX
                P.add("act", lambda e: e.activation(out=x_t[s_][:, 0, :], in_=ps[pr], func=AF.Copy), writes=[PS[pr], f"x_t{s_}r"])
                P.add("act", lambda e: e.activation(out=x_t[s_][:, 1, :], in_=ps[pi_], func=AF.Copy), writes=[PS[pi_], f"x_t{s_}i"])

            def z3(i):
                k1, half = divmod(i, 2)
                b = k1 % NKB
                s_ = i % NX
                hs = slice(half * 512, (half + 1) * 512)
                P.add("dve", lambda e: e.tensor_tensor(out=tt_[s_][:, 0, :], in0=x_t[s_][:, 0, :], in1=h_t[b][:, 0, hs], op=ALU.mult), reads=[f"x_t{s_}r", f"h_t{b}"], writes=[f"tt{s_}a"])
                P.add("dve", lambda e: e.tensor_tensor(out=tt_[s_][:, 1, :], in0=x_t[s_][:, 1, :], in1=h_t[b][:, 1, hs], op=ALU.mult), reads=[f"x_t{s_}i", f"h_t{b}"], writes=[f"tt{s_}b"])
                P.add("dve", lambda e: e.tensor_tensor(out=tt_[s_][:, 2, :], in0=x_t[s_][:, 0, :], in1=h_t[b][:, 1, hs], op=ALU.mult), reads=[f"x_t{s_}r", f"h_t{b}"], writes=[f"tt{s_}c"])
                P.add("dve", lambda e: e.tensor_tensor(out=tt_[s_][:, 3, :], in0=x_t[s_][:, 1, :], in1=h_t[b][:, 0, hs], op=ALU.mult), reads=[f"x_t{s_}i", f"h_t{b}"], writes=[f"tt{s_}d"])

            def z4(i):
                s_ = i % NX
                P.add("dve", lambda e: e.tensor_tensor(out=y_t[s_][:, 0, :], in0=tt_[s_][:, 0, :], in1=tt_[s_][:, 1, :], op=ALU.subtract), reads=[f"tt{s_}a", f"tt{s_}b"], writes=[f"y_t{s_}r"])
                P.add("dve", lambda e: e.tensor_tensor(out=y_t[s_][:, 1, :], in0=tt_[s_][:, 2, :], in1=tt_[s_][:, 3, :], op=ALU.add), reads=[f"tt{s_}c", f"tt{s_}d"], writes=[f"y_t{s_}i"])

            def z5(i):
                k1, half = divmod(i, 2)
                b = k1 % NKB
                s_ = i % NX
                pr, pi_ = gbanks(i)
                rd = [f"y_t{s_}r", f"y_t{s_}i", f"cb{b}"]
                P.add("pe", lambda e: e.matmul(ps[pr], lhsT=cb_sb[b][:, BRT], rhs=y_t[s_][:, 0, :], start=True, stop=False), reads=rd, writes=[PS[pr]])
                P.add("pe", lambda e: e.matmul(ps[pr], lhsT=cb_sb[b][:, BIT], rhs=y_t[s_][:, 1, :], start=False, stop=True), reads=rd, writes=[PS[pr]])
                P.add("pe", lambda e: e.matmul(ps[pi_], lhsT=cb_sb[b][:, BRT], rhs=y_t[s_][:, 1, :], start=True, stop=False), reads=rd, writes=[PS[pi_]])
                P.add("pe", lambda e: e.matmul(ps[pi_], lhsT=cb_sb[b][:, NBIT], rhs=y_t[s_][:, 0, :], start=False, stop=True), reads=rd, writes=[PS[pi_]])

            def z6(i):
                k1, half = divmod(i, 2)
                b = k1 % NKB
                pr, pi_ = gbanks(i)
                hs = slice(half * 512, (half + 1) * 512)
                P.add("act", lambda e: e.activation(out=g_t[b][:, 0, hs], in_=ps[pr], func=AF.Copy), writes=[PS[pr], f"g_t{b}r"])
                P.add("act", lambda e: e.activation(out=g_t[b][:, 1, hs], in_=ps[pi_], func=AF.Copy), writes=[PS[pi_], f"g_t{b}i"])
                if half == 1:
                    P.add("pool", lambda e: e.dma_start(out=G_s[k1], in_=g_t[b][:, 0, :]), reads=[f"g_t{b}r"], writes=[f"G{k1}"], dma=True)
                    P.add("pool", lambda e: e.dma_start(out=G_s[64 + k1], in_=g_t[b][:, 1, :]), reads=[f"g_t{b}i"], writes=[f"G{64 + k1}"], dma=True)

            yield from pipeline(128, [z0, fwd_b, z2, z3, z4, z5, z6])
            G_ALL = [f"G{m}" for m in range(128)]
            convv = conv_s.rearrange("(n1 n2) c -> n1 (n2 c)", n1=32)
            for blk in range(128 // NB):
                bi = cA[0] % 2
                cA[0] += 1
                P.add("sp", lambda e, bi=bi, blk=blk: e.dma_start(out=a_out[bi], in_=G_s[:, blk * NB:(blk + 1) * NB, :].rearrange("m n c -> m (n c)")), reads=G_ALL, writes=[f"a_out{bi}"], dma=True)
                for cc in range(NB * 2):
                    pb = HBK[(blk * NB * 2 + cc) % NBK]
                    csl = slice(cc * 512, (cc + 1) * 512)
                    P.add("pe", lambda e, pb=pb, bi=bi, csl=csl: e.matmul(ps[pb][0:32, :], lhsT=fai_bf, rhs=a_out[bi][:, csl], start=True, stop=True), reads=["fai_bf", f"a_out{bi}"], writes=[PS[pb]])
                    if cc % 2 == 0:
                        P.add("act", lambda e, pb=pb, bi=bi, csl=csl: e.activation(out=in_sb[bi][:, csl], in_=ps[pb][0:32, :], func=AF.Copy), writes=[PS[pb], f"in_sb{bi}"])
                    else:
                        P.add("dve", lambda e, pb=pb, bi=bi, csl=csl: e.tensor_copy(out=in_sb[bi][:, csl], in_=ps[pb][0:32, :]), writes=[PS[pb], f"in_sb{bi}"])
                P.add("pool", lambda e, bi=bi, blk=blk: e.dma_start(out=convv[:, blk * NB * 1024:(blk + 1) * NB * 1024], in_=in_sb[bi]), reads=[f"in_sb{bi}"], writes=[f"conv_{blk}"], dma=True)
                yield

            CONV_ALL = [f"conv_{blk}" for blk in range(128 // NB)]
            ar.reset(hy_base)
            P.barrier()
            ctile = [ar.alloc(4 * 1024, BF16).rearrange("p (q c) -> p q c", q=4) for _ in range(2)]
            ND = 8
            zt_d = [ar.alloc(512, BF16) for _ in range(ND)]
            x0_d = [ar.alloc(512, BF16) for _ in range(ND)]
            yh = [ar.alloc(512, F32) for _ in range(ND)]
            dsq = [ar.alloc(512, BF16) for _ in range(ND)]
            drs = [ar.alloc(512, F32) for _ in range(ND)]
            dstg = [ar.alloc(512, BF16) for _ in range(ND)]
            convt = conv_s.rearrange("(g q p) c -> g p q c", q=4, p=128)
            hbs = NBK // 2

            def d0(u):
                tg, c = divmod(u, 8)
                i = u % ND
                tsl = slice(tg * 512, (tg + 1) * 512)
                if c == 0:
                    P.add("sp", lambda e: e.dma_start(out=ctile[tg % 2], in_=convt[tg]), reads=CONV_ALL, writes=[f"ctile{tg % 2}"], dma=True)
                P.add("sp", lambda e: e.dma_start(out=zt_d[i], in_=zTv[c][:, tsl]), reads=[f"zT_s{c}_{tg}"], writes=[f"zt_d{i}"], dma=True)
                P.add("sp", lambda e: e.dma_start(out=x0_d[i], in_=x0Tv[c][:, tsl]), reads=[f"x0T_s{c}_{tg}"], writes=[f"x0_d{i}"], dma=True)

            def d1(u):
                tg, c = divmod(u, 8)
                pb = HBK[u % hbs]
                pv_ = ps[pb].bitcast(BF16)
                for q in range(4):
                    P.add("pe", lambda e, q=q: e.transpose(pv_[:, q * 128:(q + 1) * 128], ctile[tg % 2][:, q, c * 128:(c + 1) * 128], ident),
                          reads=[f"ctile{tg % 2}", "ident"], writes=[PS[pb]])

            def d2(u):
                tg, c = divmod(u, 8)
                i = u % ND
                pb = HBK[u % hbs]
                pv_ = ps[pb].bitcast(BF16)
                hb_ = VC["hbias"] + c
                P.add("dve", lambda e: e.scalar_tensor_tensor(out=yh[i], in0=zt_d[i], scalar=vecs[:, hb_:hb_ + 1], in1=pv_[:, 0:512], op0=ALU.mult, op1=ALU.add),
                      reads=[f"zt_d{i}", "vecs"], writes=[PS[pb], f"yh{i}"])

            def d3(u):
                i = u % ND
                P.add("pool", lambda e: e.tensor_tensor(out=yh[i], in0=yh[i], in1=x0_d[i], op=ALU.mult), reads=[f"x0_d{i}"], writes=[f"yh{i}"])

            def d4(u):
                i = u % ND
                P.add("act", lambda e: e.activation(out=dsq[i], in_=yh[i], func=AF.Square), reads=[f"yh{i}"], writes=[f"dsq{i}"])

            def d5(u):
                i = u % ND
                pb2 = HBK[hbs + u % hbs]
                P.add("pe", lambda e: e.matmul(ps[pb2], lhsT=ones_bf, rhs=dsq[i], start=True, stop=True), reads=[f"dsq{i}", "ones"], writes=[PS[pb2]])

            def d6(u):
                i = u % ND
                pb2 = HBK[hbs + u % hbs]
                P.add("act", lambda e: e.activation(out=drs[i], in_=ps[pb2], func=AF.Ln, scale=1.0 / 128, bias=EPS), writes=[PS[pb2], f"drs{i}"])
                P.add("act", lambda e: e.activation(out=drs[i], in_=drs[i], func=AF.Exp, scale=-0.5), writes=[f"drs{i}"])

            def d7(u):
                tg, c = divmod(u, 8)
                i = u % ND
                go = VC["hy_on"] + c
                tsl = slice(tg * 512, (tg + 1) * 512)
                P.add("dve", lambda e: e.scalar_tensor_tensor(out=dstg[i], in0=yh[i], scalar=vecs[:, go:go + 1], in1=drs[i], op0=ALU.mult, op1=ALU.mult),
                      reads=[f"yh{i}", f"drs{i}", "vecs"], writes=[f"dstg{i}"])
                P.add("pool", lambda e: e.dma_start(out=yT_s[1024 + c * 128:1024 + (c + 1) * 128, tsl], in_=dstg[i]), reads=[f"dstg{i}"], writes=[f"y_s{8 + c}_{tg}"], dma=True)

            yield from pipeline(64, [d0, d1, d2, d3, d4, d5, d6, d7])

        gens = [attn_thread(), hyena_thread()]
        if il_mode == "seq":
            for _ in gens[0]:
                pass
            P.barrier()
            for _ in gens[1]:
                pass
        elif il_mode == "attn":
            for _ in gens[0]:
                pass
        elif il_mode == "hyena":
            for _ in gens[1]:
                pass
        else:
            alive = [True, True]
            burst = (3, 4)
            while any(alive):
                for gi, g_ in enumerate(gens):
                    for _ in range(burst[gi]):
                        if alive[gi]:
                            try:
                                next(g_)
                            except StopIteration:
                                alive[gi] = False
    if stop_after <= 2:
        return nc, P
    P.barrier()
    arena.reset(base_mark)

    T = 512
    wpp_sb = arena.alloc(2 * D, BF16).rearrange("p (k n) -> p k n", k=2)
    h_sb = arena.alloc(16 * T, F32).rearrange("p (k t) -> p k t", k=16)
    yact = arena.alloc(44 * T, BF16)
    y_sb = yact[:, 0:16 * T].rearrange("p (k t) -> p k t", k=16)
    act_sb = yact.rearrange("p (k t) -> p k t", k=44)
    hn = arena.alloc(16 * T, BF16).rearrange("p (k t) -> p k t", k=16)
    p_sb = arena.alloc(2 * T, F32).rearrange("p (k t) -> p k t", k=2)
    p_bf = arena.alloc(2 * T, BF16).rearrange("p (k t) -> p k t", k=2)
    sq3 = [arena.alloc(T, BF16) for _ in range(2)]
    rstd3 = arena.alloc(T, F32)
    rstd_e = arena.alloc(T, F32)
    tmpa = [arena.alloc(T, F32) for _ in range(2)]
    tmpb = [arena.alloc(T, F32) for _ in range(2)]
    NSL = 4
    slabs3 = [arena.alloc(16 * 512, BF16) for _ in range(NSL)]
    st3 = {"slab": 0, "pair": 0, "sq": 0, "tmp": 0}
    PAIRS = [(0, 1), (2, 3), (6, 7)]
    yTv = yT_s.rearrange("(k p) t -> p k t", p=128)
    outv = outT.rearrange("(k p) t -> p k t", p=128)
    pTv = pT.rearrange("(k p) t -> p k t", p=128)
    P.add("sp", lambda e: e.dma_start(out=wpp_sb, in_=wb_pp.rearrange("(k p) n -> p k n", p=128)), reads=["wb_pp"], writes=["wpp_sb"], dma=True)

    def next_slab():
        i = st3["slab"] % NSL
        st3["slab"] += 1
        return i

    def next_pair():
        i = st3["pair"] % len(PAIRS)
        st3["pair"] += 1
        return PAIRS[i]

    def rms_stats(src_chunks, src_toks, dst_rstd, dst_tok, nfeat):
        n = len(src_chunks)
        for i, (ap_, tk) in enumerate(zip(src_chunks, src_toks)):
            b = st3["sq"] % 2
            st3["sq"] += 1
            P.add("act", lambda e, ap_=ap_, b=b: e.activation(out=sq3[b], in_=ap_, func=AF.Square), reads=[tk], writes=[f"sq3{b}"])
            P.add("pe", lambda e, b=b, i=i: e.matmul(ps[4], lhsT=ones_bf, rhs=sq3[b], start=(i == 0), stop=(i == n - 1)),
                  reads=[f"sq3{b}", "ones"], writes=[PS[4]])
        P.add("act", lambda e: e.activation(out=dst_rstd, in_=ps[4], func=AF.Ln, scale=1.0 / nfeat, bias=EPS), writes=[PS[4], dst_tok])
        P.add("act", lambda e: e.activation(out=dst_rstd, in_=dst_rstd, func=AF.Exp, scale=-0.5), writes=[dst_tok])

    pend3 = []

    def flush3():
        while pend3:
            pend3.pop(0)()

    def stat_chunk(c):
        b = st3["sq"] % 2
        st3["sq"] += 1
        P.add("act", lambda e: e.activation(out=sq3[b], in_=h_sb[:, c, :], func=AF.Square), reads=[f"h{c}"], writes=[f"sq3{b}"])
        pend3.append(lambda: P.add("pe", lambda e: e.matmul(ps[4], lhsT=ones_bf, rhs=sq3[b], start=(c == 0), stop=(c == 15)),
                                   reads=[f"sq3{b}", "ones"], writes=[PS[4]]))

    sqe = [arena.alloc(T, BF16) for _ in range(2)]

    def e_unit(c):
        pb = next_pair()[c % 2]
        for k2 in range(2):
            P.add("pe", lambda e, k2=k2: e.matmul(ps[pb], lhsT=wpp_sb[:, k2, c * 128:(c + 1) * 128], rhs=p_bf[:, k2, :], start=(k2 == 0), stop=(k2 == 1)),
                  reads=["wpp_sb", "p_bf"], writes=[PS[pb]])
        b = c % 2
        P.add("act", lambda e: e.activation(out=sqe[b], in_=ps[pb], func=AF.Square), writes=[PS[pb], f"sqe{b}"])
        pend3.append(lambda: P.add("pe", lambda e: e.matmul(ps[5], lhsT=ones_bf, rhs=sqe[b], start=(c == 0), stop=(c == 15)),
                                   reads=[f"sqe{b}", "ones"], writes=[PS[5]]))

    def norm_to_hn(gname, stats_done=False):
        g0 = VC[gname]
        if stats_done:
            flush3()
            P.add("act", lambda e: e.activation(out=rstd3, in_=ps[4], func=AF.Ln, scale=1.0 / D, bias=EPS), writes=[PS[4], "rstd3"])
            P.add("act", lambda e: e.activation(out=rstd3, in_=rstd3, func=AF.Exp, scale=-0.5), writes=["rstd3"])
        else:
            rms_stats([h_sb[:, kc, :] for kc in range(16)], [f"h{kc}" for kc in range(16)], rstd3, "rstd3", D)
        for kc in range(16):
            P.add("dve", lambda e, kc=kc: e.scalar_tensor_tensor(out=hn[:, kc, :], in0=h_sb[:, kc, :], scalar=vecs[:, g0 + kc:g0 + kc + 1], in1=rstd3,
                                                                  op0=ALU.mult, op1=ALU.mult),
                  reads=[f"h{kc}", "rstd3", "vecs"], writes=[f"hn{kc}"])

    HN = [f"hn{kc}" for kc in range(16)]
    for t in range(S // T):
        tsl = slice(t * T, (t + 1) * T)
        P.add("sp", lambda e, tsl=tsl: e.dma_start(out=y_sb, in_=yTv[:, :, tsl]), reads=[f"y_s{c}_{t}" for c in range(16)], writes=["yact"], dma=True)
        P.add("sp", lambda e, tsl=tsl: e.dma_start(out=p_sb, in_=pTv[:, :, tsl]), writes=["p_sb"], dma=True)
        P.add("dve", lambda e: e.tensor_copy(out=p_bf, in_=p_sb), reads=["p_sb"], writes=["p_bf"])
        for og in range(4):
            si = next_slab()
            sl = slabs3[si].rearrange("p (k n) -> p k n", k=16)
            P.add("sp", lambda e, sl=sl, og=og: e.dma_start(out=sl, in_=wb_out[og].rearrange("(k p) n -> p k n", p=128)),
                  reads=[f"wb_out{og}"], writes=[f"sl3{si}"], dma=True)
            if og == 0:
                for kc in range(16):
                    P.add("sp", lambda e, kc=kc, tsl=tsl: e.dma_start(out=h_sb[:, kc, :], in_=xTv[:, kc, tsl]), writes=[f"h{kc}"], dma=True)
            for m in range(4):
                c = og * 4 + m
                pb = next_pair()[m % 2]
                for kc in range(16):
                    P.add("pe", lambda e, sl=sl, kc=kc, m=m, pb=pb: e.matmul(ps[pb], lhsT=sl[:, kc, m * 128:(m + 1) * 128], rhs=y_sb[:, kc, :],
                                                                            start=(kc == 0), stop=(kc == 15)),
                          reads=[f"sl3{si}", "yact"], writes=[PS[pb]])
                flush3()
                P.add("dve", lambda e, c=c, pb=pb: e.tensor_tensor(out=h_sb[:, c, :], in0=ps[pb], in1=h_sb[:, c, :], op=ALU.add), writes=[PS[pb], f"h{c}"])
                stat_chunk(c)
                e_unit(c)
        norm_to_hn("norm2", stats_done=True)
        P.add("act", lambda e: e.activation(out=rstd_e, in_=ps[5], func=AF.Ln, scale=1.0 / D, bias=EPS), writes=[PS[5], "rstd_e"])
        P.add("act", lambda e: e.activation(out=rstd_e, in_=rstd_e, func=AF.Exp, scale=-0.5), writes=["rstd_e"])
        for j in range(22):
            si = next_slab()
            sl = slabs3[si].rearrange("p (k n) -> p k n", k=16)
            P.add("sp", lambda e, sl=sl, j=j: e.dma_start(out=sl, in_=wb_gu[j].rearrange("(k p) n -> p k n", p=128)),
                  reads=[f"wb_gua{j}", f"wb_gug{j}"], writes=[f"sl3{si}"], dma=True)
            for half in range(2):
                c = 2 * j + half
                pa, pg = next_pair()
                for kc in range(16):
                    P.add("pe", lambda e, sl=sl, kc=kc, half=half, pa=pa: e.matmul(ps[pa], lhsT=sl[:, kc, half * 128:(half + 1) * 128], rhs=hn[:, kc, :],
                                                                                  start=(kc == 0), stop=(kc == 15)),
                          reads=[f"sl3{si}", HN[kc]], writes=[PS[pa]])
                for kc in range(16):
                    P.add("pe", lambda e, sl=sl, kc=kc, half=half, pg=pg: e.matmul(ps[pg], lhsT=sl[:, kc, 256 + half * 128:256 + (half + 1) * 128], rhs=hn[:, kc, :],
                                                                                  start=(kc == 0), stop=(kc == 15)),
                          reads=[f"sl3{si}", HN[kc]], writes=[PS[pg]])
                tb = st3["tmp"] % 2
                st3["tmp"] += 1
                P.add("act", lambda e, pa=pa, tb=tb: e.activation(out=tmpa[tb], in_=ps[pa], func=AF.Silu), writes=[PS[pa], f"tmpa{tb}"])
                P.add("dve", lambda e, pg=pg, tb=tb, c=c: e.tensor_tensor(out=act_sb[:, c, :], in0=ps[pg], in1=tmpa[tb], op=ALU.mult),
                      reads=[f"tmpa{tb}"], writes=[PS[pg], "yact" if c < 16 else f"act{c}"])
        for og in range(8):
            pbs = next_pair()
            for kh in range(2):
                si = next_slab()
                sl = slabs3[si][:, 0:22 * 256].rearrange("p (k n) -> p k n", k=22)
                P.add("sp", lambda e, sl=sl, og=og, kh=kh: e.dma_start(out=sl, in_=wb_down[og, kh * 2816:(kh + 1) * 2816, :].rearrange("(k p) n -> p k n", p=128)),
                      reads=[f"wb_down{og}"], writes=[f"sl3{si}"], dma=True)
                for m in range(2):
                    for k2 in range(22):
                        kc = kh * 22 + k2
                        P.add("pe", lambda e, sl=sl, k2=k2, kc=kc, m=m, pb=pbs[m]: e.matmul(ps[pb], lhsT=sl[:, k2, m * 128:(m + 1) * 128], rhs=act_sb[:, kc, :],
                                                                                           start=(kc == 0), stop=(kc == 43)),
                              reads=[f"sl3{si}", "yact" if kc < 16 else f"act{kc}"], writes=[PS[pbs[m]]])
            flush3()
            for m in range(2):
                c = og * 2 + m
                P.add("dve", lambda e, c=c, pb=pbs[m]: e.tensor_tensor(out=h_sb[:, c, :], in0=ps[pb], in1=h_sb[:, c, :], op=ALU.add), writes=[PS[pbs[m]], f"h{c}"])
                stat_chunk(c)
        flush3()
        norm_to_hn("ple_norm", stats_done=True)
        gp = VC["ple_post"]
        for og in range(4):
            si = next_slab()
            sl = slabs3[si].rearrange("p (k n) -> p k n", k=16)
            P.add("sp", lambda e, sl=sl, og=og: e.dma_start(out=sl, in_=wb_pg[og].rearrange("(k p) n -> p k n", p=128)),
                  reads=[f"wb_pg{og}"], writes=[f"sl3{si}"], dma=True)
            for m in range(4):
                c = og * 4 + m
                pgt, pe_ = next_pair()
                for kc in range(16):
                    P.add("pe", lambda e, sl=sl, kc=kc, m=m, pgt=pgt: e.matmul(ps[pgt], lhsT=sl[:, kc, m * 128:(m + 1) * 128], rhs=hn[:, kc, :],
                                                                              start=(kc == 0), stop=(kc == 15)),
                          reads=[f"sl3{si}", HN[kc]], writes=[PS[pgt]])
                for k2 in range(2):
                    P.add("pe", lambda e, c=c, k2=k2, pe_=pe_: e.matmul(ps[pe_], lhsT=wpp_sb[:, k2, c * 128:(c + 1) * 128], rhs=p_bf[:, k2, :], start=(k2 == 0), stop=(k2 == 1)),
                          reads=["wpp_sb", "p_bf"], writes=[PS[pe_]])
                tb = st3["tmp"] % 2
                st3["tmp"] += 1
                P.add("act", lambda e, pgt=pgt, tb=tb: e.activation(out=tmpa[tb], in_=ps[pgt], func=AF.Sigmoid), writes=[PS[pgt], f"tmpa{tb}"])
                P.add("dve", lambda e, pe_=pe_, tb=tb, c=c: e.scalar_tensor_tensor(out=tmpb[tb], in0=ps[pe_], scalar=vecs[:, gp + c:gp + c + 1], in1=rstd_e,
                                                                                  op0=ALU.mult, op1=ALU.mult),
                      reads=["rstd_e", "vecs"], writes=[PS[pe_], f"tmpb{tb}"])
                P.add("pool", lambda e, tb=tb: e.tensor_tensor(out=tmpb[tb], in0=tmpb[tb], in1=tmpa[tb], op=ALU.mult), reads=[f"tmpa{tb}"], writes=[f"tmpb{tb}"])
                P.add("pool", lambda e, tb=tb, c=c: e.tensor_tensor(out=h_sb[:, c, :], in0=h_sb[:, c, :], in1=tmpb[tb], op=ALU.add), reads=[f"tmpb{tb}"], writes=[f"h{c}"])
                P.add("pool", lambda e, c=c, tsl=tsl: e.dma_start(out=outv[:, c, tsl], in_=h_sb[:, c, :]), reads=[f"h{c}"], dma=True, final=True)
    return nc, P


def _colvec(v, ncol):
    return np.ascontiguousarray(np.asarray(v, np.float32).reshape(ncol, -1).T)


def _t5_bucket(rel):
    rel = np.asarray(rel, np.int64)
    n = np.abs(rel)
    with np.errstate(divide="ignore"):
        large = 8 + (np.log(np.maximum(n, 1).astype(np.float32) / np.float32(8)) / np.float32(math.log(1024 / 8)) * np.float32(8)).astype(np.int32)
    large = np.minimum(large, 15)
    return np.where(rel > 0, 16, 0) + np.where(n < 8, n, large)


def attn_consts():
    LT = 384
    oh = np.zeros((32, 3 * LT), np.float32)
    valid = np.zeros((8, 3 * LT), np.float32)
    for gi, d in enumerate((1, 4, 16)):
        u = np.arange(129)
        b = _t5_bucket(d * (64 - u))
        oh[b, gi * LT + u] = 1.0
        valid[:, gi * LT:gi * LT + 129] = 1.0
    sel = np.zeros((8, 8, 128), np.float32)
    for h in range(8):
        sel[h, h, :] = 1.0
    return oh, valid, np.ascontiguousarray(sel.reshape(8, 8 * 128)), ((valid - 1.0) * 60.0).astype(np.float32)


def hyena_consts():
    L = S
    pos = np.arange(L, dtype=np.float64)
    t = pos / (L - 1)
    fr = np.linspace(1e-4, 15, 16)
    ang = (2.0 * math.pi / L) * pos[:, None] * fr[None, :]
    zf = np.concatenate([t[:, None], np.cos(ang), -np.sin(ang)], axis=-1).T.astype(np.float32)
    offs = (np.abs(pos - (L // 2)) / (L / 2)).astype(np.float32)
    n1 = np.arange(32)[:, None]
    k1 = np.arange(64)[None, :]
    th = 2 * math.pi * n1 * k1 / 64.0
    fa = np.concatenate([np.cos(th), -np.sin(th)], axis=1).astype(np.float32)
    n1p = np.arange(16, 48)[None, :]
    k1c = np.arange(64)[:, None]
    th2 = 2 * math.pi * n1p * k1c / 64.0
    fai = (np.concatenate([np.cos(th2), -np.sin(th2)], axis=0) / 8192.0).astype(np.float32)
    cb = np.zeros((64, 128, 768), np.float32)
    n2 = np.arange(128)[:, None].astype(np.float64)
    k2 = np.arange(128)[None, :].astype(np.float64)
    for k in range(64):
        th3 = 2 * math.pi * n2 * (k + 64 * k2) / 8192.0
        br, bi = np.cos(th3), -np.sin(th3)
        cb[k, :, 0:128] = br
        cb[k, :, 128:256] = bi
        cb[k, :, 256:384] = -bi
        cb[k, :, 384:512] = br.T
        cb[k, :, 512:640] = bi.T
        cb[k, :, 640:768] = -bi.T
    return zf, offs, fa, fai, cb


def make_in_maps(inputs):
    g = {k: np.asarray(v) for k, v in inputs.items()}
    c_oh, c_valid, c_sel, c_neg = attn_consts()
    c_zf, c_offs, c_fa, c_fai, c_b = hyena_consts()
    vec = np.zeros((128, NV), np.float32)

    def put(name, arr):
        arr = np.asarray(arr, np.float32)
        vec[:arr.shape[0], VC[name]:VC[name] + arr.shape[1]] = arr

    put("norm1", _colvec(g["norm1"][0], 16))
    put("norm2", _colvec(g["norm2"][0], 16))
    put("ple_norm", _colvec(g["ple_norm"][0], 16))
    put("ple_post", _colvec(g["ple_post_norm"][0], 16))
    put("attn_on", _colvec(g["attn_out_norm"][0], 8))
    put("hy_on", _colvec(g["hy_out_norm"][0], 8))
    put("qn", _colvec(g["q_norm"][0], 1))
    put("kn", _colvec(g["k_norm"][0], 1))
    for j in range(3):
        put(f"cw{j}", _colvec(g["conv_w"][0, j], 24))
    put("cb", _colvec(g["conv_b"][0], 24))
    put("decay", _colvec(g["hy_decay"][0], 8))
    put("hbias", _colvec(g["hy_bias"][0], 8))
    put("b1", _colvec(g["hy_b1"][0], 1))
    put("bi0", _colvec(g["hy_bi"][0, 0], 1))
    put("bi1", _colvec(g["hy_bi"][0, 1], 1))
    put("freq", _colvec(g["hy_freq"][0], 1))
    shared = {
        "w_in": np.ascontiguousarray(g["w_in"][0]), "w_out": np.ascontiguousarray(g["w_out"][0]),
        "w_gu": np.ascontiguousarray(g["w_gu"][0]), "w_down": np.ascontiguousarray(g["w_down"][0]),
        "w_pg": np.ascontiguousarray(g["w_ple_gate"][0]), "w_pp": np.ascontiguousarray(g["w_ple_proj"][0]),
        "vecs": vec, "rel_bias": np.ascontiguousarray(g["rel_bias"], dtype=np.float32),
        "c_oh": c_oh, "c_valid": c_valid, "c_sel": c_sel, "c_neg": c_neg,
        "hy_w1": np.ascontiguousarray(g["hy_w1"][0], dtype=np.float32), "hy_wi": np.ascontiguousarray(g["hy_wi"][0], dtype=np.float32),
        "hy_wo": np.ascontiguousarray(g["hy_wo"][0], dtype=np.float32),
        "c_zf": c_zf, "c_offs": c_offs, "c_fa": c_fa, "c_fai": c_fai, "c_b": c_b,
    }
    maps = []
    for b in range(8):
        m = dict(shared)
        m["xT"] = np.ascontiguousarray(g["x"][b].T)
        m["xN"] = np.ascontiguousarray(g["x"][b].reshape(32, 128, 16, 128).transpose(0, 3, 2, 1).reshape(32, 128, 2048))
        m["pT"] = np.ascontiguousarray(g["p"][0, b].T)
        maps.append(m)
    return maps


def kernel(**inputs):
    nc, P = build_program()
    P.build()
    in_maps = make_in_maps(inputs)
    res = run_bass_kernel_spmd(nc, in_maps, core_ids=list(range(8)))
    out = np.stack([np.ascontiguousarray(r["outT"].T) for r in res.results], axis=0)
    return out.astype(np.float32)
```
